# Optimizing a Trainium2 kernel written in Bass

```python
import jax, jax.numpy as jnp
from jax import lax
import numpy as np

D_MODEL = 4096
BATCH = 4
SEQ = 2048
DEPTH = 1

N_ATTN_HEADS = 16
HEAD_DIM = 128
ATTN_WIDTH = N_ATTN_HEADS * HEAD_DIM
MOBA_BLOCK = 256
MOBA_TOPK = 3
QUERY_BLOCK = 128
ROPE_THETA = 10000.0
LRU_WIDTH = 2048
LRU_BLOCKS = 16
LRU_BLOCK_WIDTH = LRU_WIDTH // LRU_BLOCKS
LRU_C = 8.0
CONV_WIDTH = 4
D_FF = 11008
MACARON_WEIGHT = 0.5
NORM_EPS = 1e-6
IN_SIZES = (ATTN_WIDTH, ATTN_WIDTH, ATTN_WIDTH, LRU_WIDTH, LRU_WIDTH, D_MODEL, D_MODEL)
IN_COLS = int(sum(IN_SIZES))
IN_SPLITS = tuple(int(s) for s in np.cumsum(IN_SIZES)[:-1])

kernel_name = "hybrid_moba_rglru_macaron_block"


def rms_norm(x, g):
    xf = x.astype(jnp.float32)
    y = xf * lax.rsqrt(jnp.mean(xf * xf, axis=-1, keepdims=True) + NORM_EPS)
    return (y * g.astype(jnp.float32)).astype(x.dtype)


def swiglu(x, w_gate, w_up, w_down):
    return (jax.nn.silu(x @ w_gate) * (x @ w_up)) @ w_down


def apply_rope(t):
    s, hd = t.shape[1], t.shape[-1]
    inv_freq = ROPE_THETA ** (-jnp.arange(0, hd, 2, dtype=jnp.float32) / hd)
    ang = jnp.arange(s, dtype=jnp.float32)[:, None] * inv_freq[None, :]
    cos = jnp.cos(ang)[None, :, None, :]
    sin = jnp.sin(ang)[None, :, None, :]
    tf = t.astype(jnp.float32)
    t1, t2 = tf[..., : hd // 2], tf[..., hd // 2:]
    return jnp.concatenate([t1 * cos - t2 * sin, t2 * cos + t1 * sin], axis=-1).astype(t.dtype)


def moba_attention(q, k, v):
    b, s, h, hd = q.shape
    s_pad = -(-s // MOBA_BLOCK) * MOBA_BLOCK
    pad = ((0, 0), (0, s_pad - s), (0, 0), (0, 0))
    q, k, v = jnp.pad(q, pad), jnp.pad(k, pad), jnp.pad(v, pad)
    nb = s_pad // MOBA_BLOCK
    nqb = s_pad // QUERY_BLOCK
    q_per_kblock = MOBA_BLOCK // QUERY_BLOCK
    topk = min(MOBA_TOPK, nb)
    scale = hd ** -0.5

    kb = k.reshape(b, nb, MOBA_BLOCK, h, hd).transpose(3, 0, 1, 2, 4)
    vb = v.reshape(b, nb, MOBA_BLOCK, h, hd).transpose(3, 0, 1, 2, 4)
    k_mean = jnp.mean(kb.astype(jnp.float32), axis=3)
    qh = q.transpose(2, 0, 1, 3)

    n_past = jnp.arange(s_pad) // MOBA_BLOCK
    past = jnp.arange(nb)[None, :] < n_past[:, None]
    gate = jnp.einsum('hbsd,hbnd->hbsn', qh.astype(jnp.float32), k_mean)
    gate = jnp.where(past, gate, -jnp.inf)
    _, sel = lax.top_k(gate, topk)
    valid = sel < n_past[:, None]

    def to_steps(t):
        t = t.reshape(h, b, nqb, QUERY_BLOCK, *t.shape[3:])
        t = jnp.moveaxis(t, 2, 1)
        return t.reshape(h * nqb, b, QUERY_BLOCK, *t.shape[4:])

    head_idx = jnp.repeat(jnp.arange(h), nqb)
    qblk_idx = jnp.tile(jnp.arange(nqb), h)
    b_idx = jnp.arange(b)[:, None, None]
    n_sel = topk * MOBA_BLOCK

    def step(args):
        hi, qi, qs, ss, vs = args
        kh, vh = kb[hi], vb[hi]
        k_sel = kh[b_idx, ss]
        v_sel = vh[b_idx, ss]
        j = qi // q_per_kblock
        k_own = lax.dynamic_index_in_dim(kh, j, axis=1, keepdims=False)
        v_own = lax.dynamic_index_in_dim(vh, j, axis=1, keepdims=False)
        s_sel = jnp.einsum('bqd,bqkcd->bqkc', qs, k_sel, preferred_element_type=jnp.float32) * scale
        s_sel = jnp.where(vs[..., None], s_sel, -jnp.inf).reshape(b, QUERY_BLOCK, n_sel)
        q_pos = qi * QUERY_BLOCK + jnp.arange(QUERY_BLOCK)
        k_pos = j * MOBA_BLOCK + jnp.arange(MOBA_BLOCK)
        s_own = jnp.einsum('bqd,bcd->bqc', qs, k_own, preferred_element_type=jnp.float32) * scale
        s_own = jnp.where(k_pos[None, None, :] <= q_pos[None, :, None], s_own, -jnp.inf)
        p = jax.nn.softmax(jnp.concatenate([s_sel, s_own], axis=-1), axis=-1)
        p_sel = p[..., :n_sel].reshape(b, QUERY_BLOCK, topk, MOBA_BLOCK).astype(v.dtype)
        p_own = p[..., n_sel:].astype(v.dtype)
        return (jnp.einsum('bqkc,bqkcd->bqd', p_sel, v_sel)
                + jnp.einsum('bqc,bcd->bqd', p_own, v_own))

    out = lax.map(step, (head_idx, qblk_idx, to_steps(qh), to_steps(sel), to_steps(valid)))
    out = out.reshape(h, nqb, b, QUERY_BLOCK, hd).transpose(2, 1, 3, 0, 4).reshape(b, s_pad, h * hd)
    return out[:, :s]


def causal_depthwise_conv(x, w, bias):
    s = x.shape[1]
    xp = jnp.pad(x, ((0, 0), (CONV_WIDTH - 1, 0), (0, 0)))
    y = bias
    for tap in range(CONV_WIDTH):
        y = y + xp[:, tap:tap + s] * w[tap]
    return y


def _linear_combine(c1, c2):
    a1, b1 = c1
    a2, b2 = c2
    return a1 * a2, a2 * b1 + b2


def rg_lru(x, w_a, b_a, w_x, b_x, lam):
    b, s, w = x.shape
    xb = x.reshape(b, s, LRU_BLOCKS, LRU_BLOCK_WIDTH)
    r = jax.nn.sigmoid((jnp.einsum('bsnc,ncd->bsnd', xb, w_a).reshape(b, s, w) + b_a).astype(jnp.float32))
    i = jax.nn.sigmoid((jnp.einsum('bsnc,ncd->bsnd', xb, w_x).reshape(b, s, w) + b_x).astype(jnp.float32))
    log_a = -LRU_C * r * jax.nn.softplus(-lam.astype(jnp.float32))
    a = jnp.exp(log_a)
    u = x.astype(jnp.float32) * i * jnp.sqrt(-jnp.expm1(2.0 * log_a))
    _, hs = lax.associative_scan(_linear_combine, (a, u), axis=1)
    return hs.astype(x.dtype)


def hybrid_mixer(hn, w_in, conv_w, conv_b, rg_w_a, rg_b_a, rg_w_x, rg_b_x, lru_lambda,
                 w_attn_out, w_rec_out, w_o):
    b, s, _ = hn.shape
    proj = hn @ w_in
    q, k, v, x_rec, x_gate, g_a, g_b = jnp.split(proj, IN_SPLITS, axis=-1)
    q = apply_rope(q.reshape(b, s, N_ATTN_HEADS, HEAD_DIM))
    k = apply_rope(k.reshape(b, s, N_ATTN_HEADS, HEAD_DIM))
    v = v.reshape(b, s, N_ATTN_HEADS, HEAD_DIM)
    y_a = moba_attention(q, k, v) @ w_attn_out
    x_rec = causal_depthwise_conv(x_rec, conv_w, conv_b)
    y_rec = rg_lru(x_rec, rg_w_a, rg_b_a, rg_w_x, rg_b_x, lru_lambda) * jax.nn.gelu(x_gate)
    y_b = y_rec @ w_rec_out
    merged = jax.nn.sigmoid(g_a) * y_a + jax.nn.sigmoid(g_b) * y_b
    return merged @ w_o


def setup_inputs(seed: int = 0) -> dict:
    key = jax.random.key(seed)
    ks = jax.random.split(key, 26)
    f32 = jnp.float32

    def normal(k, shape, fan_in):
        return jax.random.normal(k, (DEPTH,) + shape, f32) * (fan_in ** -0.5)

    def gain(k):
        return 1.0 + 0.05 * jax.random.normal(k, (DEPTH, D_MODEL), f32)

    def small(k, shape):
        return 0.01 * jax.random.normal(k, (DEPTH,) + shape, f32)

    a8 = jax.random.uniform(ks[14], (DEPTH, LRU_WIDTH), f32, 0.9, 0.999)
    a_base = a8 ** (1.0 / LRU_C)
    lru_lambda = jnp.log(a_base) - jnp.log1p(-a_base)
    return {
        'x': jax.random.normal(ks[0], (BATCH, SEQ, D_MODEL), f32),
        'ffn1_pre_g': gain(ks[1]),
        'ffn1_w_gate': normal(ks[2], (D_MODEL, D_FF), D_MODEL),
        'ffn1_w_up': normal(ks[3], (D_MODEL, D_FF), D_MODEL),
        'ffn1_w_down': normal(ks[4], (D_FF, D_MODEL), D_FF),
        'ffn1_post_g': gain(ks[5]),
        'mix_pre_g': gain(ks[6]),
        'w_in': normal(ks[7], (D_MODEL, IN_COLS), D_MODEL),
        'conv_w': normal(ks[8], (CONV_WIDTH, LRU_WIDTH), CONV_WIDTH),
        'conv_b': small(ks[9], (LRU_WIDTH,)),
        'rg_w_a': normal(ks[10], (LRU_BLOCKS, LRU_BLOCK_WIDTH, LRU_BLOCK_WIDTH), LRU_BLOCK_WIDTH),
        'rg_b_a': small(ks[11], (LRU_WIDTH,)),
        'rg_w_x': normal(ks[12], (LRU_BLOCKS, LRU_BLOCK_WIDTH, LRU_BLOCK_WIDTH), LRU_BLOCK_WIDTH),
        'rg_b_x': small(ks[13], (LRU_WIDTH,)),
        'lru_lambda': lru_lambda,
        'w_attn_out': normal(ks[15], (ATTN_WIDTH, D_MODEL), ATTN_WIDTH),
        'w_rec_out': normal(ks[16], (LRU_WIDTH, D_MODEL), LRU_WIDTH),
        'w_o': normal(ks[17], (D_MODEL, D_MODEL), D_MODEL),
        'mix_post_g': gain(ks[18]),
        'ffn2_pre_g': gain(ks[19]),
        'ffn2_w_gate': normal(ks[20], (D_MODEL, D_FF), D_MODEL),
        'ffn2_w_up': normal(ks[21], (D_MODEL, D_FF), D_MODEL),
        'ffn2_w_down': normal(ks[22], (D_FF, D_MODEL), D_FF),
        'ffn2_post_g': gain(ks[23]),
    }


def reference(x, ffn1_pre_g, ffn1_w_gate, ffn1_w_up, ffn1_w_down, ffn1_post_g,
              mix_pre_g, w_in, conv_w, conv_b, rg_w_a, rg_b_a, rg_w_x, rg_b_x, lru_lambda,
              w_attn_out, w_rec_out, w_o, mix_post_g,
              ffn2_pre_g, ffn2_w_gate, ffn2_w_up, ffn2_w_down, ffn2_post_g):
    for l in range(DEPTH):
        f = swiglu(rms_norm(x, ffn1_pre_g[l]), ffn1_w_gate[l], ffn1_w_up[l], ffn1_w_down[l])
        x = x + MACARON_WEIGHT * rms_norm(f, ffn1_post_g[l])
        m = hybrid_mixer(rms_norm(x, mix_pre_g[l]), w_in[l], conv_w[l], conv_b[l],
                         rg_w_a[l], rg_b_a[l], rg_w_x[l], rg_b_x[l], lru_lambda[l],
                         w_attn_out[l], w_rec_out[l], w_o[l])
        x = x + rms_norm(m, mix_post_g[l])
        f = swiglu(rms_norm(x, ffn2_pre_g[l]), ffn2_w_gate[l], ffn2_w_up[l], ffn2_w_down[l])
        x = x + MACARON_WEIGHT * rms_norm(f, ffn2_post_g[l])
    return x
```

```python
import numpy as np
import concourse.bass as bass
import concourse.mybir as mybir
from concourse.bass_utils import run_bass_kernel_spmd

F32 = mybir.dt.float32
BF16 = mybir.dt.bfloat16
AF = mybir.ActivationFunctionType
ALU = mybir.AluOpType
AX = mybir.AxisListType

D = 4096
DFF = 11008
NTOK = 1024
TT = 512
NH = 16
HD = 128
AW = 2048
LW = 2048
INC = 18432
EPS = 1e-6
NEG = -30000.0
NEGBIG = -1.0e30

MODE = "full"
NCORES = 8
DEBUG = False


class Tile:
    __slots__ = ("name", "writers", "readers")

    def __init__(self, name):
        self.name = name
        self.writers = {}
        self.readers = {}


class Op:
    __slots__ = ("eng", "fn", "deps", "signal", "count", "agent", "is_dma", "inc")

    def __init__(self, eng, fn):
        self.eng = eng
        self.fn = fn
        self.deps = []
        self.signal = False
        self.count = 0
        self.agent = eng
        self.is_dma = False
        self.inc = 16


ENGS = ["pe", "act", "dve", "pool", "sp"]
SAME_ENGINE_SYNC = True


class Prog:
    def __init__(self, nc):
        self.nc = nc
        self.streams = {e: [] for e in ENGS}
        self.dma_counts = {}
        self.final = []

    def _add_dep(self, o, d, raw):
        if d is o:
            return
        if d.is_dma and o.is_dma and d.agent == o.agent:
            return
        if (not d.is_dma) and d.agent == o.eng:
            if o.eng in ("pe", "sp") or not raw or not SAME_ENGINE_SYNC or o.is_dma:
                return
        if not d.is_dma:
            d.signal = True
        o.deps.append(d)

    def _track(self, o, reads, writes):
        for t in reads:
            for d in t.writers.values():
                self._add_dep(o, d, True)
        for t in writes:
            for d in t.writers.values():
                self._add_dep(o, d, False)
            for d in t.readers.values():
                self._add_dep(o, d, False)
        for t in reads:
            t.readers[o.agent] = o
        for t in writes:
            if t.readers:
                t.writers = {o.agent: o}
                t.readers = {}
            else:
                t.writers[o.agent] = o

    def op(self, eng, fn, reads=(), writes=()):
        o = Op(eng, fn)
        self._track(o, reads, writes)
        self.streams[eng].append(o)
        return o

    def dma(self, queue, key, fn, reads=(), writes=(), inc=16):
        o = Op(queue, fn)
        o.is_dma = True
        o.inc = inc
        o.agent = "dma:" + key
        c = self.dma_counts.get(key, 0) + inc
        self.dma_counts[key] = c
        o.count = c
        self._track(o, reads, writes)
        self.streams[queue].append(o)
        return o

    def barrier(self, tiles):
        for t in tiles:
            t.writers = {}
            t.readers = {}

    def emit(self):
        nc = self.nc
        for e in ENGS:
            c = 0
            for o in self.streams[e]:
                if not o.is_dma and o.signal:
                    c += 1
                    o.count = c
        from contextlib import ExitStack
        with ExitStack() as es:
            sems = {}
            for e in ENGS:
                sems[e] = es.enter_context(nc.semaphore("sem_" + e))
            for k in self.dma_counts:
                sems["dma:" + k] = es.enter_context(nc.semaphore("dsem_" + k))
            block = es.enter_context(nc.Block())

            def run(ename, eng):
                waited = {}
                for o in self.streams[ename]:
                    need = {}
                    for d in o.deps:
                        if need.get(d.agent, 0) < d.count:
                            need[d.agent] = d.count
                    for a, cnt in need.items():
                        if waited.get(a, 0) < cnt:
                            eng.wait_ge(sems[a], cnt)
                            waited[a] = cnt
                    ins = o.fn(eng)
                    if o.is_dma:
                        ins.then_inc(sems[o.agent], o.inc)
                    elif o.signal:
                        ins.then_inc(sems[ename], 1)
                if ename == "sp":
                    for d in self.final:
                        if waited.get(d.agent, 0) < d.count:
                            eng.wait_ge(sems[d.agent], d.count)
                            waited[d.agent] = d.count

            @block.tensor
            def _(eng):
                run("pe", eng)

            @block.scalar
            def _(eng):
                run("act", eng)

            @block.vector
            def _(eng):
                run("dve", eng)

            @block.gpsimd
            def _(eng):
                run("pool", eng)

            @block.sync
            def _(eng):
                run("sp", eng)


class Builder:
    def __init__(self):
        self.nc = bass.Bass("TRN2", target_bir_lowering=False)
        self.P = Prog(self.nc)
        self.nw = 0
        self.in_names = []

    def din(self, name, shape, dt=F32):
        self.in_names.append(name)
        return self.nc.dram_tensor(name, list(shape), dt, kind="ExternalInput")

    def sb(self, name, shape, dt):
        return self.nc.alloc_sbuf_tensor(name, list(shape), dt)

    def build(self):
        nc, P = self.nc, self.P
        SCALE = float(HD) ** -0.5

        x = self.din("x", [NTOK, D])
        g_names = ["ffn1_pre_g", "ffn1_post_g", "mix_pre_g", "mix_post_g", "ffn2_pre_g", "ffn2_post_g"]
        gs = {n: self.din(n, [1, D]) for n in g_names}
        gT_d = self.din("gT", [128, 6 * 32])
        if MODE == "full":
            w1g = self.din("ffn1_w_gate", [D, DFF])
            w1u = self.din("ffn1_w_up", [D, DFF])
            w1d = self.din("ffn1_w_down", [DFF, D])
            w2g = self.din("ffn2_w_gate", [D, DFF])
            w2u = self.din("ffn2_w_up", [D, DFF])
            w2d = self.din("ffn2_w_down", [DFF, D])
        w_in = self.din("w_in", [D, INC])
        w_ao = self.din("w_attn_out", [AW, D])
        w_ro = self.din("w_rec_out", [LW, D])
        w_o = self.din("w_o", [D, D])
        rgwa = self.din("rg_w_a", [16, 128, 128])
        rgwx = self.din("rg_w_x", [16, 128, 128])
        lruP_d = self.din("lruP", [128, 8 * 16])
        ident_d = self.din("ident", [128, 128], BF16)
        ones_d = self.din("ones", [128, 128], BF16)
        rot_d = self.din("rotm", [128, 128])
        tri_d = self.din("tri", [128, 128])
        cos_d = self.din("cosT", [128, NTOK])
        sin_d = self.din("sinT", [128, NTOK])
        pastb_d = self.din("pastb", [128, 64])
        flag_d = self.din("flag", [128, 1])
        out = nc.dram_tensor("out", [NTOK, D], F32, kind="ExternalOutput")

        f1_scr = nc.dram_tensor("f1_scr", [NTOK, D], F32)
        f2_scr = nc.dram_tensor("f2_scr", [NTOK, D], F32)
        dbg_kind = dict(kind="ExternalOutput") if DEBUG else {}
        x1_scr = nc.dram_tensor("x1_scr", [NTOK, D], F32, **dbg_kind)
        x2_scr = nc.dram_tensor("x2_scr", [NTOK, D], F32, **dbg_kind)
        m_scr = nc.dram_tensor("m_scr", [NTOK, D], F32, **dbg_kind)
        qT_scr = nc.dram_tensor("qT_scr", [AW, NTOK], BF16)
        kT_scr = [nc.dram_tensor(f"kT_scr{i}", [1024, NTOK], BF16) for i in range(2)]
        v_scr = [nc.dram_tensor(f"v_scr{i}", [512, AW], BF16) for i in range(2)]
        xr_scr = [nc.dram_tensor(f"xr_scr{i}", [512, NTOK], F32) for i in range(4)]
        gg_scr = nc.dram_tensor("gg_scr", [LW, NTOK], F32)
        sga_scr = nc.dram_tensor("sga_scr", [D, NTOK], F32)
        sgb_scr = nc.dram_tensor("sgb_scr", [D, NTOK], F32)
        kT_all = [nc.dram_tensor(f"kT_all{i}", [2048, NTOK], BF16) for i in range(2)]
        v_all = [nc.dram_tensor(f"v_all{i}", [1024, AW], BF16) for i in range(2)]
        xr_all = [nc.dram_tensor(f"xr_all{i}", [1024, NTOK], F32) for i in range(4)]

        self.NWS = 4
        wsl = [self.sb(f"wslot{i}", [128, 8192], BF16) for i in range(self.NWS)]
        wsl_t = [Tile(f"wslot{i}") for i in range(self.NWS)]
        hnT = self.sb("hnT", [128, 32, TT], BF16)
        hnT_t = Tile("hnT")
        big = self.sb("big", [128, 44032], BF16)
        hT = big[:, :].rearrange("p (c t) -> p c t", t=TT)
        hT_t = [Tile(f"hT{c}") for c in range(86)]
        ident = self.sb("ident_sb", [128, 128], BF16)
        ones = self.sb("ones_sb", [128, 128], BF16)
        rotm = self.sb("rot_sb", [128, 128], F32)
        tri = self.sb("tri_sb", [128, 128], F32)
        pastb = self.sb("pastb_sb", [128, 64], F32)
        flag = self.sb("flag_sb", [128, 1], F32)
        gT = self.sb("gT_sb", [128, 6 * 32], F32)
        lp = self.sb("lp_sb", [128, 8 * 16], F32)
        lsm = self.sb("lsm_sb", [128, 8 * 16], F32)
        wax = self.sb("wax_sb", [128, 2, 16, 128], BF16)
        small = self.sb("small", [128, 64], F32)
        ag = self.sb("ag", [128, 2, 192], F32)
        amx = self.sb("amx", [128, 2, 16], F32)
        sg = [self.sb(f"sg{i}", [128, TT], F32) for i in range(2)]
        fst = [self.sb(f"fst{i}", [128, TT], F32) for i in range(2)]
        const_t = Tile("consts")
        sg_t = [Tile("sg0"), Tile("sg1")]
        fst_t = [Tile("fst0"), Tile("fst1")]
        sm_t = Tile("small")
        ag_t = [Tile("ag0"), Tile("ag1")]
        amx_t = [Tile("amx0"), Tile("amx1")]
        lsc_t = Tile("lsc")
        lsm_t = Tile("lsm")
        ps = [nc.alloc_psum_tensor(f"ps{i}", [128, 512], F32) for i in range(8)]
        ps_t = [Tile(f"ps{i}") for i in range(8)]

        def bview(lo, n):
            return big[:, lo:lo + n]

        def fview(lo, n):
            return big[:, lo:lo + 2 * n].bitcast(F32)

        def wb(i, lo, n):
            return wsl[i][:, lo:lo + n]

        def wf(i, lo, n):
            return wsl[i][:, lo:lo + 2 * n].bitcast(F32)

        xt = [fview(0, D), fview(8192, D)]
        ft0 = fview(16384, D)
        gb0 = fview(24576, D)
        xn = bview(32768, 4096)
        junk = bview(36864, 4096)
        xt_t = [Tile("xt0"), Tile("xt1")]
        ft_t = Tile("ft0")
        gb_t = Tile("gb0")
        xn_t = Tile("xn")
        junk_t = Tile("junk")
        stf = [fview(1024 * i, 512) for i in range(6)]
        stf_t = [Tile(f"stf{i}") for i in range(6)]
        stb = [bview(6144 + 512 * i, 512) for i in range(4)]
        stb_t = [Tile(f"stb{i}") for i in range(4)]
        cosb = fview(8192, 512)
        sinb = fview(9216, 512)
        cs_t = Tile("cossin")
        attnT = bview(0, 16384).rearrange("p (c t) -> p c t", t=NTOK)
        yrecT = bview(16384, 16384).rearrange("p (c t) -> p c t", t=NTOK)
        attnT_t = [Tile(f"attnT{h}") for h in range(16)]
        yrecT_t = [Tile(f"yrecT{c}") for c in range(16)]
        otA = fview(32768, 512)
        otB = fview(33792, 512)
        osg = [fview(34816 + 1024 * i, 512) for i in range(4)]
        otA_t, otB_t = Tile("otA"), Tile("otB")
        osg_t = [Tile(f"osg{i}") for i in range(4)]
        big_tiles = (hT_t + xt_t + [ft_t, gb_t, xn_t, junk_t] + stf_t + stb_t + [cs_t] + attnT_t + yrecT_t
                     + [otA_t, otB_t] + osg_t)

        KT = [wb(i, 0, 2048) for i in range(2)]
        VV = [wb(i, 2048, 2048).rearrange("p (c d) -> p c d", d=128) for i in range(2)]
        QT = [wb(i, 4096, 1024) for i in range(2)]
        kvq_t = [Tile("kvq0"), Tile("kvq1")]
        SmL = [wf(2, 0, 2048), wf(3, 0, 2048)]
        PbL = [wb(2, 4096, 2048), wb(3, 4096, 2048)]
        PTL = [wb(2, 6144, 2048), wb(3, 6144, 2048)]
        Sm_t, Pb_t, PT_t = ([Tile(f"{n}{i}") for i in range(2)] for n in ("Sm", "Pb", "PT"))
        kmb = [wb(i, 5120, 8) for i in range(2)]
        rcL = [wf(0, 5136, 128), wf(1, 5136, 128)]
        rc_t = [Tile("rc0"), Tile("rc1")]
        xr = wf(3, 0, 2051)
        yb = wb(3, 4104, 512)
        ggb = wf(3, 4616, 512)
        ly = wf(3, 5640, 512)
        lr = wf(3, 6664, 512)
        li = wf(1, 5136, 512)
        lb1 = wf(1, 6160, 512)
        lb2 = wf(0, 5392 + 256, 512)
        lh = wf(0, 5392 + 256 + 1024, 512)
        xr_t, yb_t, ggb_t, ly_t, lr_t, li_t, lb1_t, lb2_t, lh_t = [Tile(n) for n in
                                                                   ["xr", "yb", "ggb", "ly", "lr", "li", "lb1", "lb2", "lh"]]
        wsl_tiles = (wsl_t + kvq_t + Sm_t + Pb_t + PT_t + rc_t + [xr_t, yb_t, ggb_t, ly_t, lr_t, li_t, lb1_t, lb2_t, lh_t])

        def guard_big():
            P.op("dve", lambda e: e.memset(small[:, 60:61], 0.0), reads=[], writes=big_tiles)

        def guard_wsl():
            P.op("dve", lambda e: e.memset(small[:, 61:62], 0.0), reads=[], writes=wsl_tiles)

        def ACT(out_, in_, func, r, w, **kw):
            P.op("act", lambda e: e.activation(out=out_, in_=in_, func=func, **kw), reads=r, writes=w)

        def TS(out_, in0, s1, s2, op0, op1, r, w, eng="dve", **kw):
            if op1 is None:
                P.op(eng, lambda e: e.tensor_scalar(out=out_, in0=in0, scalar1=s1, scalar2=None, op0=op0, **kw),
                     reads=r, writes=w)
            else:
                P.op(eng, lambda e: e.tensor_scalar(out=out_, in0=in0, scalar1=s1, scalar2=s2, op0=op0, op1=op1, **kw),
                     reads=r, writes=w)

        def TTo(out_, in0, in1, op, r, w, eng="dve"):
            P.op(eng, lambda e: e.tensor_tensor(out=out_, in0=in0, in1=in1, op=op), reads=r, writes=w)

        def STT(out_, in0, scalar, in1, op0, op1, r, w):
            P.op("dve", lambda e: e.scalar_tensor_tensor(out=out_, in0=in0, scalar=scalar, in1=in1, op0=op0, op1=op1),
                 reads=r, writes=w)

        def MM(out_, lhsT, rhs, start, stop, r, w):
            P.op("pe", lambda e: e.matmul(out_, lhsT=lhsT, rhs=rhs, start=start, stop=stop), reads=r, writes=w)

        def TR(out_, in_, r, w):
            P.op("pe", lambda e: e.transpose(out=out_, in_=in_, identity=ident[:, :]), reads=r, writes=w)

        def DMA(key, out_, in_, r, w, queue="sp"):
            return P.dma(queue, key, lambda e: e.dma_start(out=out_, in_=in_), reads=r, writes=w)

        def COPY(eng, out_, in_, r, w):
            if eng == "act":
                P.op("act", lambda e: e.copy(out=out_, in_=in_), reads=r, writes=w)
            else:
                P.op(eng, lambda e: e.tensor_copy(out=out_, in_=in_), reads=r, writes=w)

        dram_tiles = {}

        def dt_(name):
            if name not in dram_tiles:
                dram_tiles[name] = Tile(name)
            return dram_tiles[name]

        state = {"ng": 0, "ev": 0, "st": 0, "sb": 0}

        def wslot():
            i = self.nw % self.NWS
            self.nw += 1
            return i

        for (dst, src) in ((ident, ident_d), (ones, ones_d), (rotm, rot_d), (tri, tri_d), (pastb, pastb_d),
                           (flag, flag_d), (gT, gT_d), (lp, lruP_d)):
            DMA("const", dst[:, :], src[:, :], [], [const_t])
        DMA("constw", wax[:, 0, :, :], rgwa.ap().rearrange("n c d -> c n d"), [], [const_t], queue="pool")
        DMA("constw", wax[:, 1, :, :], rgwx.ap().rearrange("n c d -> c n d"), [], [const_t], queue="pool")

        def rstd_chain(src_col, c0, wgt):
            TS(small[:, c0 + 1:c0 + 2], small[:, src_col:src_col + 1], 1.0 / D, EPS, ALU.mult, ALU.add, [sm_t], [sm_t])
            ACT(small[:, c0 + 2:c0 + 3], small[:, c0 + 1:c0 + 2], AF.Sqrt, [sm_t], [sm_t])
            P.op("dve", lambda e: e.reciprocal(out=small[:, c0 + 3:c0 + 4], in_=small[:, c0 + 2:c0 + 3]),
                 reads=[sm_t], writes=[sm_t])
            if wgt != 1.0:
                TS(small[:, c0 + 3:c0 + 4], small[:, c0 + 3:c0 + 4], float(wgt), None, ALU.mult, None, [sm_t], [sm_t])

        def norm_phase(t, x_src, f_src, gpost, wgt, x_dst, gidx, xname, fname, dname, final=False):
            guard_big()
            if f_src is not None:
                DMA("gb0", gb0, gpost[0:1, :].partition_broadcast(128), [], [gb_t])
            for tg in range(4):
                r0 = t * TT + tg * 128
                i = state["ng"] % 2
                state["ng"] += 1
                DMA(f"xt{i}", xt[i], x_src[r0:r0 + 128, :], [dt_(f"{xname}_{r0}")], [xt_t[i]])
                c0 = 16 * (tg % 2)
                if f_src is not None:
                    DMA("ft0", ft0, f_src[r0:r0 + 128, :], [dt_(f"{fname}_{r0}")], [ft_t])
                    ACT(junk, ft0, AF.Square, [ft_t], [junk_t, sm_t], accum_out=small[:, c0:c0 + 1])
                    rstd_chain(c0, c0, wgt)
                    TTo(ft0, ft0, gb0, ALU.mult, [ft_t, gb_t], [ft_t])
                    STT(xt[i], ft0, small[:, c0 + 3:c0 + 4], xt[i], ALU.mult, ALU.add, [ft_t, xt_t[i], sm_t], [xt_t[i]])
                    st = DMA(f"xst{i}", x_dst[r0:r0 + 128, :], xt[i], [xt_t[i]], [dt_(f"{dname}_{r0}")])
                    if final:
                        P.final.append(st)
                if gidx is not None:
                    ACT(junk, xt[i], AF.Square, [xt_t[i]], [junk_t, sm_t], accum_out=small[:, c0 + 8:c0 + 9])
                    rstd_chain(c0 + 8, c0 + 8, 1.0)
                    TS(xn, xt[i], small[:, c0 + 11:c0 + 12], None, ALU.mult, None, [xt_t[i], sm_t], [xn_t])
                    for j in range(4):
                        pb = ps[j][:, :].bitcast(BF16)
                        for q in range(8):
                            kc = 8 * j + q
                            TR(pb[:, q * 128:(q + 1) * 128], xn[:, kc * 128:(kc + 1) * 128], [xn_t, const_t], [ps_t[j]])
                        for q in range(8):
                            kc = 8 * j + q
                            src = pb[:, q * 128:(q + 1) * 128]
                            dst = hnT[:, kc, tg * 128:(tg + 1) * 128]
                            gcol = gT[:, gidx * 32 + kc:gidx * 32 + kc + 1]
                            if q % 2 == 0:
                                ACT(dst, src, AF.Copy, [ps_t[j], const_t], [hnT_t], scale=gcol)
                            else:
                                TS(dst, src, gcol, None, ALU.mult, None, [ps_t[j], const_t], [hnT_t])

        def proj_tm(srcT, src_tiles, nkc, wmat, dst, dname, t):
            kblocks = [(k0, min(16, nkc - k0)) for k0 in range(0, nkc, 16)]
            for bi in range(D // 512):
                base = 4 * (bi % 2)
                for (k0, nk) in kblocks:
                    si = wslot()
                    v = wsl[si][:, :].rearrange("p (k c) -> p k c", c=512)
                    src = wmat[k0 * 128:(k0 + nk) * 128, bi * 512:(bi + 1) * 512].rearrange("(k p) c -> p k c", p=128)
                    DMA(f"w{si}", v[:, 0:nk, :], src, [], [wsl_t[si]], queue="pool")
                    for kk in range(nk):
                        kc = k0 + kk
                        for tg in range(4):
                            MM(ps[base + tg][:, :], srcT[:, kc, tg * 128:(tg + 1) * 128], v[:, kk, :],
                               kc == 0, kc == nkc - 1, [wsl_t[si], src_tiles(kc)], [ps_t[base + tg]])
                for tg in range(4):
                    k = state["ev"] % 2
                    state["ev"] += 1
                    r0 = t * TT + tg * 128
                    COPY("act" if tg % 2 == 0 else "dve", fst[k][:, :], ps[base + tg][:, :], [ps_t[base + tg]], [fst_t[k]])
                    DMA(f"fst{k}", dst[r0:r0 + 128, bi * 512:(bi + 1) * 512], fst[k][:, :], [fst_t[k]],
                        [dt_(f"{dname}_{r0}")])

        def ffn(t, wg, wu, wd, f_dst, fname):
            guard_big()
            for j in range(DFF // 256):
                base = 4 * (j % 2)
                slots = []
                for half in range(2):
                    si = wslot()
                    slots.append(si)
                    v = wsl[si][:, :].rearrange("p (k g c) -> p k g c", g=2, c=256)
                    for gu, w in enumerate((wg, wu)):
                        src = w[half * 2048:(half + 1) * 2048, j * 256:(j + 1) * 256].rearrange("(k p) c -> p k c", p=128)
                        DMA(f"w{si}", v[:, :, gu, :], src, [], [wsl_t[si]], queue="pool")
                for half in range(2):
                    si = slots[half]
                    v = wsl[si][:, :].rearrange("p (k g c) -> p k g c", g=2, c=256)
                    for c in range(2):
                        for gu in range(2):
                            b = base + gu * 2 + c
                            for kk in range(16):
                                MM(ps[b][:, :], v[:, kk, gu, c * 128:(c + 1) * 128], hnT[:, half * 16 + kk, :],
                                   half == 0 and kk == 0, half == 1 and kk == 15, [wsl_t[si], hnT_t], [ps_t[b]])
                for c in range(2):
                    k = state["ev"] % 2
                    state["ev"] += 1
                    ACT(sg[k][:, :], ps[base + c][:, :], AF.Silu, [ps_t[base + c]], [sg_t[k]])
                    TTo(hT[:, 2 * j + c, :], sg[k][:, :], ps[base + 2 + c][:, :], ALU.mult,
                        [sg_t[k], ps_t[base + 2 + c]], [hT_t[2 * j + c]])
            proj_tm(hT, lambda kc: hT_t[kc], 86, wd, f_dst, fname, t)

        def nstf():
            i = state["st"] % 6
            state["st"] += 1
            return i

        def nstb():
            i = state["sb"] % 4
            state["sb"] += 1
            return i

        def rope_store(bank, rbank, dst_scr, row0, t, dname):
            a, b1, b2 = nstf(), nstf(), nstf()
            ACT(stf[a], ps[bank][:, :], AF.Copy, [ps_t[bank]], [stf_t[a]])
            MM(ps[rbank][:, :], rotm[:, :], stf[a], True, True, [stf_t[a], const_t], [ps_t[rbank]])
            TTo(stf[b1], stf[a], cosb, ALU.mult, [stf_t[a], cs_t], [stf_t[b1]])
            TTo(stf[b2], ps[rbank][:, :], sinb, ALU.mult, [ps_t[rbank], cs_t], [stf_t[b2]])
            o = nstb()
            TTo(stb[o], stf[b1], stf[b2], ALU.add, [stf_t[b1], stf_t[b2]], [stb_t[o]])
            if isinstance(dst_scr, list):
                dst_scr, row0, dname = dst_scr[row0 // 1024], row0 % 1024, f"{dname}{row0 // 1024}"
            DMA(f"stb{o}", dst_scr[row0:row0 + 128, t * TT:(t + 1) * TT], stb[o], [stb_t[o]], [dt_(dname)])

        def proj_in(t):
            guard_big()
            DMA("cs", cosb, cos_d[:, t * TT:(t + 1) * TT], [], [cs_t])
            DMA("cs", sinb, sin_d[:, t * TT:(t + 1) * TT], [], [cs_t])
            for j in range(INC // 256):
                base = 4 * (j % 2)
                si = wslot()
                v = wsl[si][:, :].rearrange("p (k c) -> p k c", c=256)
                for half in range(2):
                    src = w_in[half * 2048:(half + 1) * 2048, j * 256:(j + 1) * 256].rearrange("(k p) c -> p k c", p=128)
                    DMA(f"w{si}", v[:, half * 16:(half + 1) * 16, :], src, [], [wsl_t[si]], queue="pool")
                col0 = j * 256
                if 4096 <= col0 < 6144:
                    for tg in range(4):
                        for kk in range(32):
                            MM(ps[base + tg][:, 0:256], hnT[:, kk, tg * 128:(tg + 1) * 128], v[:, kk, :],
                               kk == 0, kk == 31, [wsl_t[si], hnT_t], [ps_t[base + tg]])
                    for tg in range(4):
                        o = nstb()
                        r0 = t * TT + tg * 128
                        COPY("act" if tg % 2 == 0 else "dve", stb[o][:, 0:256], ps[base + tg][:, 0:256],
                             [ps_t[base + tg]], [stb_t[o]])
                        DMA(f"stb{o}", v_scr[t][tg * 128:(tg + 1) * 128, col0 - 4096:col0 - 4096 + 256], stb[o][:, 0:256],
                            [stb_t[o]], [dt_(f"v_scr{t}")])
                    continue
                for c in range(2):
                    for kk in range(32):
                        MM(ps[base + c][:, :], v[:, kk, c * 128:(c + 1) * 128], hnT[:, kk, :],
                           kk == 0, kk == 31, [wsl_t[si], hnT_t], [ps_t[base + c]])
                for c in range(2):
                    col = col0 + c * 128
                    bank = base + c
                    if col < 2048:
                        rope_store(bank, base + 2 + c, qT_scr, col, t, "qT_scr")
                    elif col < 4096:
                        rope_store(bank, base + 2 + c, kT_scr, col - 2048, t, "kT_scr")
                    elif col < 8192:
                        a = nstf()
                        row0 = col - 6144
                        COPY("act" if c == 0 else "dve", stf[a], ps[bank][:, :], [ps_t[bank]], [stf_t[a]])
                        DMA(f"stf{a}", xr_scr[row0 // 512][row0 % 512:row0 % 512 + 128, t * TT:(t + 1) * TT], stf[a],
                            [stf_t[a]], [dt_(f"xr_scr{row0 // 512}")])
                    elif col < 10240:
                        a, b = nstf(), nstf()
                        row0 = col - 8192
                        ACT(stf[a], ps[bank][:, :], AF.Copy, [ps_t[bank]], [stf_t[a]])
                        TTo(stf[b], stf[a], stf[a], ALU.mult, [stf_t[a]], [stf_t[b]])
                        TS(stf[b], stf[b], 0.044715, 1.0, ALU.mult, ALU.add, [stf_t[b]], [stf_t[b]])
                        TTo(stf[b], stf[b], stf[a], ALU.mult, [stf_t[a], stf_t[b]], [stf_t[b]])
                        ACT(stf[b], stf[b], AF.Sigmoid, [stf_t[b]], [stf_t[b]], scale=1.5957691216057308)
                        TTo(stf[b], stf[b], stf[a], ALU.mult, [stf_t[a], stf_t[b]], [stf_t[b]])
                        DMA(f"stf{b}", gg_scr[row0:row0 + 128, t * TT:(t + 1) * TT], stf[b], [stf_t[b]], [dt_("gg_scr")])
                    else:
                        a = nstf()
                        if col < 14336:
                            dst, row0, dn = sga_scr, col - 10240, "sga_scr"
                        else:
                            dst, row0, dn = sgb_scr, col - 14336, "sgb_scr"
                        ACT(stf[a], ps[bank][:, :], AF.Sigmoid, [ps_t[bank]], [stf_t[a]])
                        DMA(f"stf{a}", dst[row0:row0 + 128, t * TT:(t + 1) * TT], stf[a], [stf_t[a]], [dt_(dn)])

        def exchange():
            groups = [[2 * i, 2 * i + 1] for i in range(NCORES // 2)]
            for nm, srcs, dsts in (("kT", kT_scr, kT_all), ("v", v_scr, v_all), ("xr", xr_scr, xr_all)):
                for i, (src, dst) in enumerate(zip(srcs, dsts)):
                    P.dma("pool", f"cc_{nm}{i}",
                          lambda e, src=src, dst=dst: e.collective_compute("AllGather", ALU.bypass, replica_groups=groups,
                                                                           ins=[src.ap().opt()], outs=[dst.ap().opt()]),
                          reads=[dt_(f"{nm}_scr{i}")], writes=[dt_(f"{nm}_all{i}")], inc=1)

        def attention():
            guard_big()
            guard_wsl()
            units = [(h, qi) for h in range(NH) for qi in range(8)]

            def head_setup(h):
                s = h % 2
                kt, vv, qt = KT[s], VV[s], QT[s]
                hs = slice(h * 128, (h + 1) * 128)
                hi, hr = h // 8, (h % 8) * 128
                DMA(f"kvq{s}", kt[:, 0:1024], kT_all[hi][hr:hr + 128, :], [dt_(f"kT_all{hi}")], [kvq_t[s]])
                DMA(f"kvq{s}", kt[:, 1024:2048], kT_scr[hi][hr:hr + 128, :], [dt_(f"kT_scr{hi}")], [kvq_t[s]])
                for tc in range(2):
                    DMA(f"kvq{s}", vv[:, 4 * tc:4 * tc + 4, :], v_all[tc][0:512, hs].rearrange("(c p) d -> p c d", p=128),
                        [dt_(f"v_all{tc}")], [kvq_t[s]])
                    DMA(f"kvq{s}", vv[:, 8 + 4 * tc:12 + 4 * tc, :], v_scr[tc][0:512, hs].rearrange("(c p) d -> p c d", p=128),
                        [dt_(f"v_scr{tc}")], [kvq_t[s]])
                DMA(f"kvq{s}", qt, qT_scr[h * 128:(h + 1) * 128, :], [dt_("qT_scr")], [kvq_t[s]])
                G = ag[:, s, :]
                P.op("dve", lambda e: e.tensor_reduce(out=G[:, 128:136], in_=kt.rearrange("p (b k) -> p b k", k=256),
                                                      axis=AX.X, op=ALU.add), reads=[kvq_t[s]], writes=[ag_t[s]])
                TS(kmb[s], G[:, 128:136], 1.0 / 256.0, None, ALU.mult, None, [ag_t[s]], [kvq_t[s]])
                for qi in range(8):
                    MM(ps[7][:, qi * 8:(qi + 1) * 8], qt[:, qi * 128:(qi + 1) * 128], kmb[s], True, True, [kvq_t[s]], [ps_t[7]])
                TTo(G[:, 0:64], ps[7][:, 0:64], pastb[:, 0:64], ALU.add, [ps_t[7], const_t], [ag_t[s]])
                for qi in range(8):
                    c0 = 128 + 8 * (qi % 2)
                    P.op("dve", lambda e, c0=c0, qi=qi: e.max(out=G[:, c0:c0 + 8], in_=G[:, qi * 8:(qi + 1) * 8]),
                         reads=[ag_t[s]], writes=[ag_t[s]])
                    TS(G[:, 64 + qi * 8:72 + qi * 8], G[:, qi * 8:(qi + 1) * 8], G[:, c0 + 2:c0 + 3], None, ALU.is_ge, None,
                       [ag_t[s]], [ag_t[s]])
                TS(G[:, 64:128], G[:, 64:128], -NEG, NEG, ALU.mult, ALU.add, [ag_t[s]], [ag_t[s]])
                TTo(G[:, 64:128], G[:, 64:128], pastb[:, 0:64], ALU.add, [ag_t[s], const_t], [ag_t[s]])

            def stage_a(idx):
                h, qi = units[idx]
                if qi == 0:
                    head_setup(h)
                s, p = h % 2, idx % 2
                kt, qt, G = KT[s], QT[s], ag[:, s, :]
                mx = amx[:, p, :]
                Sm_, Pb_ = SmL[p], PbL[p]
                nk = 1024 + (qi + 1) * 128
                npc = (nk + 511) // 512
                ob = 4 + qi // 2
                qs = qt[:, qi * 128:(qi + 1) * 128]
                for pc in range(npc):
                    w = min(512, nk - pc * 512)
                    MM(ps[4 * p + pc][:, 0:w], qs, kt[:, pc * 512:pc * 512 + w], True, True, [kvq_t[s]], [ps_t[4 * p + pc]])
                ncol = 0
                for blk in range(ob):
                    bank, off = 4 * p + blk // 2, (blk % 2) * 256
                    bcol = 64 + qi * 8 + blk
                    TS(Sm_[:, blk * 256:(blk + 1) * 256], ps[bank][:, off:off + 256], G[:, bcol:bcol + 1], None,
                       ALU.add, ALU.max, [ps_t[bank], ag_t[s]], [Sm_t[p], amx_t[p]], accum_out=mx[:, ncol:ncol + 1])
                    ncol += 1
                k0 = ob * 256
                if qi % 2 == 1:
                    bank, off = 4 * p + k0 // 512, k0 % 512
                    TS(Sm_[:, k0:k0 + 128], ps[bank][:, off:off + 128], 0.0, None, ALU.add, ALU.max,
                       [ps_t[bank]], [Sm_t[p], amx_t[p]], accum_out=mx[:, ncol:ncol + 1])
                    ncol += 1
                    k0 += 128
                bank, off = 4 * p + k0 // 512, k0 % 512
                TTo(Sm_[:, k0:k0 + 128], ps[bank][:, off:off + 128], tri[:, :], ALU.add, [ps_t[bank], const_t], [Sm_t[p]])
                P.op("dve", lambda e, ncol=ncol, k0=k0: e.tensor_reduce(out=mx[:, ncol:ncol + 1], in_=Sm_[:, k0:k0 + 128],
                                                                        axis=AX.X, op=ALU.max),
                     reads=[Sm_t[p]], writes=[amx_t[p]])
                ncol += 1
                P.op("dve", lambda e, ncol=ncol: e.tensor_reduce(out=mx[:, 12:13], in_=mx[:, 0:ncol], axis=AX.X, op=ALU.max),
                     reads=[amx_t[p]], writes=[amx_t[p]])
                TS(mx[:, 13:14], mx[:, 12:13], -SCALE, None, ALU.mult, None, [amx_t[p]], [amx_t[p]])
                ACT(Pb_[:, 0:nk], Sm_[:, 0:nk], AF.Exp, [Sm_t[p], amx_t[p]], [Pb_t[p]], bias=mx[:, 13:14], scale=SCALE)

            def stage_b(idx):
                h, qi = units[idx]
                s, p = h % 2, idx % 2
                vv = VV[s]
                Pb_, PT_, rc_ = PbL[p], PTL[p], rcL[p]
                nk = 1024 + (qi + 1) * 128
                nch = nk // 128
                for c in range(nch):
                    bank = 4 * p + c // 8
                    pbv = ps[bank][:, :].bitcast(BF16)
                    TR(pbv[:, (c % 8) * 128:(c % 8 + 1) * 128], Pb_[:, c * 128:(c + 1) * 128], [Pb_t[p], const_t], [ps_t[bank]])
                COPY("act", PT_[:, 0:1024], ps[4 * p][:, :].bitcast(BF16), [ps_t[4 * p]], [PT_t[p]])
                n2 = (nch - 8) * 128
                COPY("dve", PT_[:, 1024:1024 + n2], ps[4 * p + 1][:, :].bitcast(BF16)[:, 0:n2], [ps_t[4 * p + 1]], [PT_t[p]])
                bo, br = 4 * p + 2, 4 * p + 3
                for c in range(nch):
                    MM(ps[bo][:, 0:128], vv[:, c, :], PT_[:, c * 128:(c + 1) * 128], c == 0, c == nch - 1,
                       [kvq_t[s], PT_t[p]], [ps_t[bo]])
                for c in range(nch):
                    MM(ps[br][:, 0:128], ones[:, :], PT_[:, c * 128:(c + 1) * 128], c == 0, c == nch - 1,
                       [const_t, PT_t[p]], [ps_t[br]])
                P.op("dve", lambda e: e.reciprocal(out=rc_, in_=ps[br][:, 0:128]), reads=[ps_t[br]], writes=[rc_t[p]])
                TTo(attnT[:, h, qi * 128:(qi + 1) * 128], ps[bo][:, 0:128], rc_, ALU.mult, [ps_t[bo], rc_t[p]], [attnT_t[h]])

            stage_a(0)
            for i in range(len(units)):
                if i + 1 < len(units):
                    stage_a(i + 1)
                stage_b(i)

        def lru_consts():
            L = lambda i: lsm[:, 16 * i:16 * (i + 1)]
            lam = lp[:, 7 * 16:8 * 16]
            ACT(L(0), lam, AF.Exp, [const_t], [lsm_t], scale=-1.0)
            TS(L(1), L(0), 2.0, None, ALU.add, None, [lsm_t], [lsm_t])
            P.op("dve", lambda e: e.reciprocal(out=L(1), in_=L(1)), reads=[lsm_t], writes=[lsm_t])
            TTo(L(1), L(1), L(0), ALU.mult, [lsm_t], [lsm_t])
            TTo(L(2), L(1), L(1), ALU.mult, [lsm_t], [lsm_t])
            TS(L(3), L(2), 1.0 / 9.0, 1.0 / 7.0, ALU.mult, ALU.add, [lsm_t], [lsm_t])
            TTo(L(3), L(3), L(2), ALU.mult, [lsm_t], [lsm_t])
            TS(L(3), L(3), 1.0 / 5.0, None, ALU.add, None, [lsm_t], [lsm_t])
            TTo(L(3), L(3), L(2), ALU.mult, [lsm_t], [lsm_t])
            TS(L(3), L(3), 1.0 / 3.0, None, ALU.add, None, [lsm_t], [lsm_t])
            TTo(L(3), L(3), L(2), ALU.mult, [lsm_t], [lsm_t])
            TS(L(3), L(3), 1.0, None, ALU.add, None, [lsm_t], [lsm_t])
            TTo(L(3), L(3), L(1), ALU.mult, [lsm_t], [lsm_t])
            TS(L(4), L(3), 16.0, None, ALU.mult, None, [lsm_t], [lsm_t])
            TS(L(5), L(3), -16.0, None, ALU.mult, None, [lsm_t], [lsm_t])
            TS(L(6), L(3), -32.0, None, ALU.mult, None, [lsm_t], [lsm_t])

        def lru():
            guard_wsl()
            L = lambda i, c: lsm[:, 16 * i + c:16 * i + c + 1]
            LPc = lambda i, c: lp[:, 16 * i + c:16 * i + c + 1]
            P.op("dve", lambda e: e.memset(xr[:, 0:3], 0.0), reads=[], writes=[xr_t])
            for c in range(16):
                rows = slice(c * 128, (c + 1) * 128)
                ci, cr = c // 4, (c % 4) * 128
                DMA("xr", xr[:, 3:1027], xr_all[ci][cr:cr + 128, :], [dt_(f"xr_all{ci}")], [xr_t])
                DMA("xr", xr[:, 1027:2051], xr_scr[ci][cr:cr + 128, :], [dt_(f"xr_scr{ci}")], [xr_t])
                TS(xr[:, 3:1027], xr[:, 3:1027], flag[:, 0:1], None, ALU.mult, None, [xr_t, const_t], [xr_t])
                for pc in range(4):
                    o0 = pc * 512
                    if pc >= 2:
                        DMA("ggb", ggb, gg_scr[c * 128:(c + 1) * 128, (pc - 2) * 512:(pc - 1) * 512], [dt_("gg_scr")], [ggb_t])
                    TS(ly, xr[:, o0:o0 + 512], LPc(0, c), LPc(4, c), ALU.mult, ALU.add, [xr_t, const_t], [ly_t])
                    for k in range(1, 4):
                        STT(ly, xr[:, o0 + k:o0 + k + 512], LPc(k, c), ly, ALU.mult, ALU.add, [xr_t, const_t, ly_t], [ly_t])
                    COPY("act", yb, ly, [ly_t], [yb_t])
                    MM(ps[6][:, :], wax[:, 0, c, :], yb, True, True, [yb_t, const_t], [ps_t[6]])
                    MM(ps[7][:, :], wax[:, 1, c, :], yb, True, True, [yb_t, const_t], [ps_t[7]])
                    ACT(lr, ps[6][:, :], AF.Sigmoid, [ps_t[6], const_t], [lr_t], bias=LPc(5, c))
                    ACT(li, ps[7][:, :], AF.Sigmoid, [ps_t[7], const_t], [li_t], bias=LPc(6, c))
                    ACT(lb1, lr, AF.Exp, [lr_t, lsm_t], [lb1_t], scale=L(6, c))
                    ACT(lb2, lr, AF.Tanh, [lr_t, lsm_t], [lb2_t], scale=L(4, c))
                    ACT(lr, lr, AF.Exp, [lr_t, lsm_t], [lr_t], scale=L(5, c))
                    STT(lb1, lb1, 1.0, lb2, ALU.add, ALU.mult, [lb1_t, lb2_t], [lb1_t])
                    ACT(lb1, lb1, AF.Sqrt, [lb1_t], [lb1_t])
                    TTo(ly, ly, li, ALU.mult, [ly_t, li_t], [ly_t])
                    TTo(ly, ly, lb1, ALU.mult, [ly_t, lb1_t], [ly_t])
                    if pc < 2:
                        TS(ly, ly, flag[:, 0:1], None, ALU.mult, None, [ly_t, const_t], [ly_t])
                    if pc == 0:
                        P.op("dve", lambda e: e.tensor_tensor_scan(out=lh, data0=lr, data1=ly, initial=0.0,
                                                                   op0=ALU.mult, op1=ALU.add),
                             reads=[lr_t, ly_t], writes=[lh_t])
                    else:
                        TS(lsm[:, 120:121], lh[:, 511:512], 1.0, None, ALU.mult, None, [lh_t], [lsc_t])
                        P.op("dve", lambda e: e.tensor_tensor_scan(out=lh, data0=lr, data1=ly, initial=lsm[:, 120:121],
                                                                   op0=ALU.mult, op1=ALU.add),
                             reads=[lr_t, ly_t, lsc_t], writes=[lh_t])
                    if pc >= 2:
                        TTo(yrecT[:, c, (pc - 2) * 512:(pc - 1) * 512], lh, ggb, ALU.mult, [lh_t, ggb_t], [yrecT_t[c]])

        def out_proj(t):
            guard_wsl()
            tok = slice(t * TT, (t + 1) * TT)
            nsg = 0
            for bi in range(D // 256):
                base = 4 * (bi % 2)
                si = wslot()
                v = wsl[si][:, :].rearrange("p (g k c) -> p g k c", g=2, c=256)
                for g, w in enumerate((w_ao, w_ro)):
                    DMA(f"w{si}", v[:, g, :, :], w[:, bi * 256:(bi + 1) * 256].rearrange("(k p) c -> p k c", p=128), [],
                        [wsl_t[si]], queue="pool")
                for g, (srcT, tl) in enumerate(((attnT, attnT_t), (yrecT, yrecT_t))):
                    for c in range(2):
                        for kk in range(16):
                            MM(ps[base + 2 * g + c][:, :], v[:, g, kk, c * 128:(c + 1) * 128], srcT[:, kk, tok],
                               kk == 0, kk == 15, [wsl_t[si], tl[kk]], [ps_t[base + 2 * g + c]])
                for c in range(2):
                    ch = 2 * bi + c
                    ia, ib = nsg % 4, (nsg + 1) % 4
                    nsg += 2
                    DMA(f"osg{ia}", osg[ia], sga_scr[ch * 128:(ch + 1) * 128, tok], [dt_("sga_scr")], [osg_t[ia]])
                    DMA(f"osg{ib}", osg[ib], sgb_scr[ch * 128:(ch + 1) * 128, tok], [dt_("sgb_scr")], [osg_t[ib]])
                    TTo(otA, ps[base + c][:, :], osg[ia], ALU.mult, [ps_t[base + c], osg_t[ia]], [otA_t])
                    TTo(otB, ps[base + 2 + c][:, :], osg[ib], ALU.mult, [ps_t[base + 2 + c], osg_t[ib]], [otB_t])
                    TTo(hnT[:, ch, :], otA, otB, ALU.add, [otA_t, otB_t], [hnT_t])
            proj_tm(hnT, lambda kc: hnT_t, 32, w_o, m_scr, "m", t)

        full = MODE == "full"
        for t in range(NTOK // TT):
            if full:
                norm_phase(t, x, None, None, 0.0, None, 0, "x", None, None)
                ffn(t, w1g, w1u, w1d, f1_scr, "f1")
                norm_phase(t, x, f1_scr, gs["ffn1_post_g"], 0.5, x1_scr, 2, "x", "f1", "x1")
            else:
                norm_phase(t, x, None, None, 0.0, None, 2, "x", None, None)
            proj_in(t)
        exchange()
        lru_consts()
        attention()
        lru()
        for t in range(NTOK // TT):
            out_proj(t)
        for t in range(NTOK // TT):
            if full:
                norm_phase(t, x1_scr, m_scr, gs["mix_post_g"], 1.0, x2_scr, 4, "x1", "m", "x2")
                ffn(t, w2g, w2u, w2d, f2_scr, "f2")
                norm_phase(t, x2_scr, f2_scr, gs["ffn2_post_g"], 0.5, out, None, "x2", "f2", "out", final=True)
            else:
                norm_phase(t, x, m_scr, gs["mix_post_g"], 1.0, out, None, "x", "m", "out", final=True)

        P.emit()
        self.nc_in_names = list(self.in_names)
        return nc


_CACHE = {}


def _get_nc():
    key = ("nc", MODE, NCORES, DEBUG)
    if key not in _CACHE:
        b = Builder()
        _CACHE[key] = (b.build(), b.in_names)
    return _CACHE[key]


def _host_consts(s):
    import ml_dtypes
    bf = ml_dtypes.bfloat16
    c = {}
    c["ident"] = np.eye(128, dtype=np.float32).astype(bf)
    c["ones"] = np.ones((128, 128), dtype=np.float32).astype(bf)
    rot = np.zeros((128, 128), dtype=np.float32)
    for d in range(64):
        rot[d + 64, d] = -1.0
        rot[d, d + 64] = 1.0
    c["rotm"] = rot
    p = np.arange(128)
    c["tri"] = np.where(p[None, :] <= p[:, None], 0.0, NEG).astype(np.float32)
    inv_freq = (np.float32(10000.0) ** (-np.arange(0, HD, 2, dtype=np.float32) / np.float32(HD))).astype(np.float32)
    pos = (np.arange(NTOK, dtype=np.float32) + np.float32(s * NTOK)).astype(np.float32)
    ang = (pos[:, None] * inv_freq[None, :]).astype(np.float32)
    cosT = np.cos(ang).astype(np.float32).T
    sinT = np.sin(ang).astype(np.float32).T
    c["cosT"] = np.ascontiguousarray(np.concatenate([cosT, cosT], axis=0))
    c["sinT"] = np.ascontiguousarray(np.concatenate([sinT, sinT], axis=0))
    pb = np.full((8, 8), NEGBIG, dtype=np.float32)
    for qi in range(8):
        ob = 4 + qi // 2
        lo = 0 if s == 1 else 4
        pb[qi, lo:ob] = 0.0
    c["pastb"] = np.ascontiguousarray(np.broadcast_to(pb.reshape(1, 64), (128, 64))).astype(np.float32)
    c["flag"] = np.full((128, 1), float(s), dtype=np.float32)
    return c


def kernel(**inputs):
    nc, in_names = _get_nc()
    f32 = lambda a: np.asarray(a, dtype=np.float32)
    x = np.ascontiguousarray(f32(inputs["x"]))
    B, S, _ = x.shape
    g_names = ["ffn1_pre_g", "ffn1_post_g", "mix_pre_g", "mix_post_g", "ffn2_pre_g", "ffn2_post_g"]
    shared = {}
    shared["gT"] = np.ascontiguousarray(np.concatenate([f32(inputs[n]).reshape(32, 128).T for n in g_names], axis=1))
    for n in g_names:
        shared[n] = np.ascontiguousarray(f32(inputs[n]).reshape(1, D))
    for n in ["ffn1_w_gate", "ffn1_w_up", "ffn1_w_down", "ffn2_w_gate", "ffn2_w_up", "ffn2_w_down", "w_in",
              "w_attn_out", "w_rec_out", "w_o", "rg_w_a", "rg_w_x"]:
        if n not in in_names:
            continue
        a = f32(inputs[n])
        shared[n] = np.ascontiguousarray(a.reshape(a.shape[1:]))
    cw = f32(inputs["conv_w"]).reshape(4, LW)
    cols = [cw[k] for k in range(4)] + [f32(inputs[n]).reshape(LW) for n in ("conv_b", "rg_b_a", "rg_b_x", "lru_lambda")]
    shared["lruP"] = np.ascontiguousarray(np.concatenate([v.reshape(16, 128).T for v in cols], axis=1))
    consts = [_host_consts(0), _host_consts(1)]
    in_maps = []
    for c in range(NCORES):
        b, s = c // 2, c % 2
        m = dict(shared)
        m.update(consts[s])
        m["x"] = np.ascontiguousarray(x[b, s * NTOK:(s + 1) * NTOK, :])
        in_maps.append({k: v for k, v in m.items() if k in in_names})
    res = run_bass_kernel_spmd(nc, in_maps, core_ids=list(range(NCORES)))
    outp = np.zeros((B, S, D), dtype=np.float32)
    for c in range(NCORES):
        b, s = c // 2, c % 2
        outp[b, s * NTOK:(s + 1) * NTOK, :] = res.results[c]["out"]
    if DEBUG:
        _CACHE["dbg"] = res.results
    return outp
```

```python
import numpy as np
import concourse.bass as bass
import concourse.mybir as mybir
from concourse.bass_utils import run_bass_kernel_spmd

F32 = mybir.dt.float32
BF16 = mybir.dt.bfloat16
AF = mybir.ActivationFunctionType
ALU = mybir.AluOpType
AX = mybir.AxisListType

D = 4096
DFF = 11008
NTOK = 1024
TT = 512
NH = 16
HD = 128
AW = 2048
LW = 2048
INC = 18432
EPS = 1e-6
NEG = -30000.0
NEGBIG = -1.0e30

MODE = "full"
NCORES = 8
DEBUG = False


class Tile:
    __slots__ = ("name", "writers", "readers")

    def __init__(self, name):
        self.name = name
        self.writers = {}
        self.readers = {}


class Op:
    __slots__ = ("eng", "fn", "deps", "signal", "count", "agent", "is_dma", "inc")

    def __init__(self, eng, fn):
        self.eng = eng
        self.fn = fn
        self.deps = []
        self.signal = False
        self.count = 0
        self.agent = eng
        self.is_dma = False
        self.inc = 16


ENGS = ["pe", "act", "dve", "pool", "sp"]
SAME_ENGINE_SYNC = True


class Prog:
    def __init__(self, nc):
        self.nc = nc
        self.streams = {e: [] for e in ENGS}
        self.dma_counts = {}
        self.final = []

    def _add_dep(self, o, d, raw):
        if d is o:
            return
        if d.is_dma and o.is_dma and d.agent == o.agent:
            return
        if (not d.is_dma) and d.agent == o.eng:
            if o.eng in ("pe", "sp") or not raw or not SAME_ENGINE_SYNC or o.is_dma:
                return
        if not d.is_dma:
            d.signal = True
        o.deps.append(d)

    def _track(self, o, reads, writes):
        for t in reads:
            for d in t.writers.values():
                self._add_dep(o, d, True)
        for t in writes:
            for d in t.writers.values():
                self._add_dep(o, d, False)
            for d in t.readers.values():
                self._add_dep(o, d, False)
        for t in reads:
            t.readers[o.agent] = o
        for t in writes:
            if t.readers:
                t.writers = {o.agent: o}
                t.readers = {}
            else:
                t.writers[o.agent] = o

    def op(self, eng, fn, reads=(), writes=()):
        o = Op(eng, fn)
        self._track(o, reads, writes)
        self.streams[eng].append(o)
        return o

    def dma(self, queue, key, fn, reads=(), writes=(), inc=16):
        o = Op(queue, fn)
        o.is_dma = True
        o.inc = inc
        o.agent = "dma:" + key
        c = self.dma_counts.get(key, 0) + inc
        self.dma_counts[key] = c
        o.count = c
        self._track(o, reads, writes)
        self.streams[queue].append(o)
        return o

    def barrier(self, tiles):
        for t in tiles:
            t.writers = {}
            t.readers = {}

    def emit(self):
        nc = self.nc
        for e in ENGS:
            c = 0
            for o in self.streams[e]:
                if not o.is_dma and o.signal:
                    c += 1
                    o.count = c
        from contextlib import ExitStack
        with ExitStack() as es:
            sems = {}
            for e in ENGS:
                sems[e] = es.enter_context(nc.semaphore("sem_" + e))
            for k in self.dma_counts:
                sems["dma:" + k] = es.enter_context(nc.semaphore("dsem_" + k))
            block = es.enter_context(nc.Block())

            def run(ename, eng):
                waited = {}
                for o in self.streams[ename]:
                    need = {}
                    for d in o.deps:
                        if need.get(d.agent, 0) < d.count:
                            need[d.agent] = d.count
                    for a, cnt in need.items():
                        if waited.get(a, 0) < cnt:
                            eng.wait_ge(sems[a], cnt)
                            waited[a] = cnt
                    ins = o.fn(eng)
                    if o.is_dma:
                        ins.then_inc(sems[o.agent], o.inc)
                    elif o.signal:
                        ins.then_inc(sems[ename], 1)
                if ename == "sp":
                    for d in self.final:
                        if waited.get(d.agent, 0) < d.count:
                            eng.wait_ge(sems[d.agent], d.count)
                            waited[d.agent] = d.count

            @block.tensor
            def _(eng):
                run("pe", eng)

            @block.scalar
            def _(eng):
                run("act", eng)

            @block.vector
            def _(eng):
                run("dve", eng)

            @block.gpsimd
            def _(eng):
                run("pool", eng)

            @block.sync
            def _(eng):
                run("sp", eng)


class Builder:
    def __init__(self):
        self.nc = bass.Bass("TRN2", target_bir_lowering=False)
        self.P = Prog(self.nc)
        self.nw = 0
        self.in_names = []

    def din(self, name, shape, dt=F32):
        self.in_names.append(name)
        return self.nc.dram_tensor(name, list(shape), dt, kind="ExternalInput")

    def sb(self, name, shape, dt):
        return self.nc.alloc_sbuf_tensor(name, list(shape), dt)

    def build(self):
        nc, P = self.nc, self.P
        SCALE = float(HD) ** -0.5

        x = self.din("x", [NTOK, D])
        g_names = ["ffn1_pre_g", "ffn1_post_g", "mix_pre_g", "mix_post_g", "ffn2_pre_g", "ffn2_post_g"]
        gs = {n: self.din(n, [1, D]) for n in g_names}
        gT_d = self.din("gT", [128, 6 * 32])
        if MODE == "full":
            w1g = self.din("ffn1_w_gate", [D, DFF])
            w1u = self.din("ffn1_w_up", [D, DFF])
            w1d = self.din("ffn1_w_down", [DFF, D])
            w2g = self.din("ffn2_w_gate", [D, DFF])
            w2u = self.din("ffn2_w_up", [D, DFF])
            w2d = self.din("ffn2_w_down", [DFF, D])
        w_in = self.din("w_in", [D, INC])
        w_ao = self.din("w_attn_out", [AW, D])
        w_ro = self.din("w_rec_out", [LW, D])
        w_o = self.din("w_o", [D, D])
        rgwa = self.din("rg_w_a", [16, 128, 128])
        rgwx = self.din("rg_w_x", [16, 128, 128])
        lruP_d = self.din("lruP", [128, 8 * 16])
        ident_d = self.din("ident", [128, 128], BF16)
        ones_d = self.din("ones", [128, 128], BF16)
        rot_d = self.din("rotm", [128, 128])
        tri_d = self.din("tri", [128, 128])
        cos_d = self.din("cosT", [128, NTOK])
        sin_d = self.din("sinT", [128, NTOK])
        pastb_d = self.din("pastb", [128, 64])
        flag_d = self.din("flag", [128, 1])
        out = nc.dram_tensor("out", [NTOK, D], F32, kind="ExternalOutput")

        f1_scr = nc.dram_tensor("f1_scr", [NTOK, D], F32)
        f2_scr = nc.dram_tensor("f2_scr", [NTOK, D], F32)
        dbg_kind = dict(kind="ExternalOutput") if DEBUG else {}
        x1_scr = nc.dram_tensor("x1_scr", [NTOK, D], F32, **dbg_kind)
        x2_scr = nc.dram_tensor("x2_scr", [NTOK, D], F32, **dbg_kind)
        m_scr = nc.dram_tensor("m_scr", [NTOK, D], F32, **dbg_kind)
        qT_scr = nc.dram_tensor("qT_scr", [AW, NTOK], BF16)
        kT_scr = [nc.dram_tensor(f"kT_scr{i}", [1024, NTOK], BF16) for i in range(2)]
        v_scr = [nc.dram_tensor(f"v_scr{i}", [512, AW], BF16) for i in range(2)]
        xr_scr = [nc.dram_tensor(f"xr_scr{i}", [512, NTOK], F32) for i in range(4)]
        gg_scr = nc.dram_tensor("gg_scr", [LW, NTOK], F32)
        sga_scr = nc.dram_tensor("sga_scr", [D, NTOK], F32)
        sgb_scr = nc.dram_tensor("sgb_scr", [D, NTOK], F32)
        kT_all = [nc.dram_tensor(f"kT_all{i}", [2048, NTOK], BF16) for i in range(2)]
        v_all = [nc.dram_tensor(f"v_all{i}", [1024, AW], BF16) for i in range(2)]
        xr_all = [nc.dram_tensor(f"xr_all{i}", [1024, NTOK], F32) for i in range(4)]

        self.NWS = 4
        wsl = [self.sb(f"wslot{i}", [128, 8192], BF16) for i in range(self.NWS)]
        wsl_t = [Tile(f"wslot{i}") for i in range(self.NWS)]
        hnT = self.sb("hnT", [128, 32, TT], BF16)
        hnT_t = Tile("hnT")
        big = self.sb("big", [128, 44032], BF16)
        hT = big[:, :].rearrange("p (c t) -> p c t", t=TT)
        hT_t = [Tile(f"hT{c}") for c in range(86)]
        ident = self.sb("ident_sb", [128, 128], BF16)
        ones = self.sb("ones_sb", [128, 128], BF16)
        rotm = self.sb("rot_sb", [128, 128], F32)
        tri = self.sb("tri_sb", [128, 128], F32)
        pastb = self.sb("pastb_sb", [128, 64], F32)
        flag = self.sb("flag_sb", [128, 1], F32)
        gT = self.sb("gT_sb", [128, 6 * 32], F32)
        lp = self.sb("lp_sb", [128, 8 * 16], F32)
        lsm = self.sb("lsm_sb", [128, 8 * 16], F32)
        wax = self.sb("wax_sb", [128, 2, 16, 128], BF16)
        small = self.sb("small", [128, 64], F32)
        ag = self.sb("ag", [128, 2, 192], F32)
        amx = self.sb("amx", [128, 2, 16], F32)
        sg = [self.sb(f"sg{i}", [128, TT], F32) for i in range(2)]
        fst = [self.sb(f"fst{i}", [128, TT], F32) for i in range(2)]
        const_t = Tile("consts")
        sg_t = [Tile("sg0"), Tile("sg1")]
        fst_t = [Tile("fst0"), Tile("fst1")]
        sm_t = Tile("small")
        ag_t = [Tile("ag0"), Tile("ag1")]
        amx_t = [Tile("amx0"), Tile("amx1")]
        lsc_t = Tile("lsc")
        lsm_t = Tile("lsm")
        ps = [nc.alloc_psum_tensor(f"ps{i}", [128, 512], F32) for i in range(8)]
        ps_t = [Tile(f"ps{i}") for i in range(8)]

        def bview(lo, n):
            return big[:, lo:lo + n]

        def fview(lo, n):
            return big[:, lo:lo + 2 * n].bitcast(F32)

        def wb(i, lo, n):
            return wsl[i][:, lo:lo + n]

        def wf(i, lo, n):
            return wsl[i][:, lo:lo + 2 * n].bitcast(F32)

        xt = [fview(0, D), fview(8192, D)]
        ft0 = fview(16384, D)
        gb0 = fview(24576, D)
        xn = bview(32768, 4096)
        junk = bview(36864, 4096)
        xt_t = [Tile("xt0"), Tile("xt1")]
        ft_t = Tile("ft0")
        gb_t = Tile("gb0")
        xn_t = Tile("xn")
        junk_t = Tile("junk")
        stf = [fview(1024 * i, 512) for i in range(6)]
        stf_t = [Tile(f"stf{i}") for i in range(6)]
        stb = [bview(6144 + 512 * i, 512) for i in range(4)]
        stb_t = [Tile(f"stb{i}") for i in range(4)]
        cosb = fview(8192, 512)
        sinb = fview(9216, 512)
        cs_t = Tile("cossin")
        attnT = bview(0, 16384).rearrange("p (c t) -> p c t", t=NTOK)
        yrecT = bview(16384, 16384).rearrange("p (c t) -> p c t", t=NTOK)
        attnT_t = [Tile(f"attnT{h}") for h in range(16)]
        yrecT_t = [Tile(f"yrecT{c}") for c in range(16)]
        otA = fview(32768, 512)
        otB = fview(33792, 512)
        osg = [fview(34816 + 1024 * i, 512) for i in range(4)]
        otA_t, otB_t = Tile("otA"), Tile("otB")
        osg_t = [Tile(f"osg{i}") for i in range(4)]
        big_tiles = (hT_t + xt_t + [ft_t, gb_t, xn_t, junk_t] + stf_t + stb_t + [cs_t] + attnT_t + yrecT_t
                     + [otA_t, otB_t] + osg_t)

        KT = [wb(i, 0, 2048) for i in range(2)]
        VV = [wb(i, 2048, 2048).rearrange("p (c d) -> p c d", d=128) for i in range(2)]
        QT = [wb(i, 4096, 1024) for i in range(2)]
        kvq_t = [Tile("kvq0"), Tile("kvq1")]
        SmL = [wf(2, 0, 2048), wf(3, 0, 2048)]
        PbL = [wb(2, 4096, 2048), wb(3, 4096, 2048)]
        PTL = [wb(2, 6144, 2048), wb(3, 6144, 2048)]
        Sm_t, Pb_t, PT_t = ([Tile(f"{n}{i}") for i in range(2)] for n in ("Sm", "Pb", "PT"))
        kmb = [wb(i, 5120, 8) for i in range(2)]
        rcL = [wf(0, 5136, 128), wf(1, 5136, 128)]
        rc_t = [Tile("rc0"), Tile("rc1")]
        xr = wf(3, 0, 2051)
        yb = wb(3, 4104, 512)
        ggb = wf(3, 4616, 512)
        ly = wf(3, 5640, 512)
        lr = wf(3, 6664, 512)
        li = wf(1, 5136, 512)
        lb1 = wf(1, 6160, 512)
        lb2 = wf(0, 5392 + 256, 512)
        lh = wf(0, 5392 + 256 + 1024, 512)
        xr_t, yb_t, ggb_t, ly_t, lr_t, li_t, lb1_t, lb2_t, lh_t = [Tile(n) for n in
                                                                   ["xr", "yb", "ggb", "ly", "lr", "li", "lb1", "lb2", "lh"]]
        wsl_tiles = (wsl_t + kvq_t + Sm_t + Pb_t + PT_t + rc_t + [xr_t, yb_t, ggb_t, ly_t, lr_t, li_t, lb1_t, lb2_t, lh_t])

        def guard_big():
            P.op("dve", lambda e: e.memset(small[:, 60:61], 0.0), reads=[], writes=big_tiles)

        def guard_wsl():
            P.op("dve", lambda e: e.memset(small[:, 61:62], 0.0), reads=[], writes=wsl_tiles)

        def ACT(out_, in_, func, r, w, **kw):
            P.op("act", lambda e: e.activation(out=out_, in_=in_, func=func, **kw), reads=r, writes=w)

        def TS(out_, in0, s1, s2, op0, op1, r, w, eng="dve", **kw):
            if op1 is None:
                P.op(eng, lambda e: e.tensor_scalar(out=out_, in0=in0, scalar1=s1, scalar2=None, op0=op0, **kw),
                     reads=r, writes=w)
            else:
                P.op(eng, lambda e: e.tensor_scalar(out=out_, in0=in0, scalar1=s1, scalar2=s2, op0=op0, op1=op1, **kw),
                     reads=r, writes=w)

        def TTo(out_, in0, in1, op, r, w, eng="dve"):
            P.op(eng, lambda e: e.tensor_tensor(out=out_, in0=in0, in1=in1, op=op), reads=r, writes=w)

        def STT(out_, in0, scalar, in1, op0, op1, r, w):
            P.op("dve", lambda e: e.scalar_tensor_tensor(out=out_, in0=in0, scalar=scalar, in1=in1, op0=op0, op1=op1),
                 reads=r, writes=w)

        def MM(out_, lhsT, rhs, start, stop, r, w):
            P.op("pe", lambda e: e.matmul(out_, lhsT=lhsT, rhs=rhs, start=start, stop=stop), reads=r, writes=w)

        def TR(out_, in_, r, w):
            P.op("pe", lambda e: e.transpose(out=out_, in_=in_, identity=ident[:, :]), reads=r, writes=w)

        def DMA(key, out_, in_, r, w, queue="sp"):
            return P.dma(queue, key, lambda e: e.dma_start(out=out_, in_=in_), reads=r, writes=w)

        def COPY(eng, out_, in_, r, w):
            if eng == "act":
                P.op("act", lambda e: e.copy(out=out_, in_=in_), reads=r, writes=w)
            else:
                P.op(eng, lambda e: e.tensor_copy(out=out_, in_=in_), reads=r, writes=w)

        dram_tiles = {}

        def dt_(name):
            if name not in dram_tiles:
                dram_tiles[name] = Tile(name)
            return dram_tiles[name]

        state = {"ng": 0, "ev": 0, "st": 0, "sb": 0}

        def wslot():
            i = self.nw % self.NWS
            self.nw += 1
            return i

        for (dst, src) in ((ident, ident_d), (ones, ones_d), (rotm, rot_d), (tri, tri_d), (pastb, pastb_d),
                           (flag, flag_d), (gT, gT_d), (lp, lruP_d)):
            DMA("const", dst[:, :], src[:, :], [], [const_t])
        DMA("constw", wax[:, 0, :, :], rgwa.ap().rearrange("n c d -> c n d"), [], [const_t], queue="pool")
        DMA("constw", wax[:, 1, :, :], rgwx.ap().rearrange("n c d -> c n d"), [], [const_t], queue="pool")

        def rstd_chain(src_col, c0, wgt):
            TS(small[:, c0 + 1:c0 + 2], small[:, src_col:src_col + 1], 1.0 / D, EPS, ALU.mult, ALU.add, [sm_t], [sm_t])
            ACT(small[:, c0 + 2:c0 + 3], small[:, c0 + 1:c0 + 2], AF.Sqrt, [sm_t], [sm_t])
            P.op("dve", lambda e: e.reciprocal(out=small[:, c0 + 3:c0 + 4], in_=small[:, c0 + 2:c0 + 3]),
                 reads=[sm_t], writes=[sm_t])
            if wgt != 1.0:
                TS(small[:, c0 + 3:c0 + 4], small[:, c0 + 3:c0 + 4], float(wgt), None, ALU.mult, None, [sm_t], [sm_t])

        def norm_phase(t, x_src, f_src, gpost, wgt, x_dst, gidx, xname, fname, dname, final=False):
            guard_big()
            if f_src is not None:
                DMA("gb0", gb0, gpost[0:1, :].partition_broadcast(128), [], [gb_t])
            for tg in range(4):
                r0 = t * TT + tg * 128
                i = state["ng"] % 2
                state["ng"] += 1
                DMA(f"xt{i}", xt[i], x_src[r0:r0 + 128, :], [dt_(f"{xname}_{r0}")], [xt_t[i]])
                c0 = 16 * (tg % 2)
                if f_src is not None:
                    DMA("ft0", ft0, f_src[r0:r0 + 128, :], [dt_(f"{fname}_{r0}")], [ft_t])
                    ACT(junk, ft0, AF.Square, [ft_t], [junk_t, sm_t], accum_out=small[:, c0:c0 + 1])
                    rstd_chain(c0, c0, wgt)
                    TTo(ft0, ft0, gb0, ALU.mult, [ft_t, gb_t], [ft_t])
                    STT(xt[i], ft0, small[:, c0 + 3:c0 + 4], xt[i], ALU.mult, ALU.add, [ft_t, xt_t[i], sm_t], [xt_t[i]])
                    st = DMA(f"xst{i}", x_dst[r0:r0 + 128, :], xt[i], [xt_t[i]], [dt_(f"{dname}_{r0}")])
                    if final:
                        P.final.append(st)
                if gidx is not None:
                    ACT(junk, xt[i], AF.Square, [xt_t[i]], [junk_t, sm_t], accum_out=small[:, c0 + 8:c0 + 9])
                    rstd_chain(c0 + 8, c0 + 8, 1.0)
                    TS(xn, xt[i], small[:, c0 + 11:c0 + 12], None, ALU.mult, None, [xt_t[i], sm_t], [xn_t])
                    for j in range(4):
                        pb = ps[j][:, :].bitcast(BF16)
                        for q in range(8):
                            kc = 8 * j + q
                            TR(pb[:, q * 128:(q + 1) * 128], xn[:, kc * 128:(kc + 1) * 128], [xn_t, const_t], [ps_t[j]])
                        for q in range(8):
                            kc = 8 * j + q
                            src = pb[:, q * 128:(q + 1) * 128]
                            dst = hnT[:, kc, tg * 128:(tg + 1) * 128]
                            gcol = gT[:, gidx * 32 + kc:gidx * 32 + kc + 1]
                            if q % 2 == 0:
                                ACT(dst, src, AF.Copy, [ps_t[j], const_t], [hnT_t], scale=gcol)
                            else:
                                TS(dst, src, gcol, None, ALU.mult, None, [ps_t[j], const_t], [hnT_t])

        def proj_tm(srcT, src_tiles, nkc, wmat, dst, dname, t):
            kblocks = [(k0, min(16, nkc - k0)) for k0 in range(0, nkc, 16)]
            for bi in range(D // 512):
                base = 4 * (bi % 2)
                for (k0, nk) in kblocks:
                    si = wslot()
                    v = wsl[si][:, :].rearrange("p (k c) -> p k c", c=512)
                    src = wmat[k0 * 128:(k0 + nk) * 128, bi * 512:(bi + 1) * 512].rearrange("(k p) c -> p k c", p=128)
                    DMA(f"w{si}", v[:, 0:nk, :], src, [], [wsl_t[si]], queue="pool")
                    for kk in range(nk):
                        kc = k0 + kk
                        for tg in range(4):
                            MM(ps[base + tg][:, :], srcT[:, kc, tg * 128:(tg + 1) * 128], v[:, kk, :],
                               kc == 0, kc == nkc - 1, [wsl_t[si], src_tiles(kc)], [ps_t[base + tg]])
                for tg in range(4):
                    k = state["ev"] % 2
                    state["ev"] += 1
                    r0 = t * TT + tg * 128
                    COPY("act" if tg % 2 == 0 else "dve", fst[k][:, :], ps[base + tg][:, :], [ps_t[base + tg]], [fst_t[k]])
                    DMA(f"fst{k}", dst[r0:r0 + 128, bi * 512:(bi + 1) * 512], fst[k][:, :], [fst_t[k]],
                        [dt_(f"{dname}_{r0}")])

        def ffn(t, wg, wu, wd, f_dst, fname):
            guard_big()
            for j in range(DFF // 256):
                base = 4 * (j % 2)
                slots = []
                for half in range(2):
                    si = wslot()
                    slots.append(si)
                    v = wsl[si][:, :].rearrange("p (k g c) -> p k g c", g=2, c=256)
                    for gu, w in enumerate((wg, wu)):
                        src = w[half * 2048:(half + 1) * 2048, j * 256:(j + 1) * 256].rearrange("(k p) c -> p k c", p=128)
                        DMA(f"w{si}", v[:, :, gu, :], src, [], [wsl_t[si]], queue="pool")
                for half in range(2):
                    si = slots[half]
                    v = wsl[si][:, :].rearrange("p (k g c) -> p k g c", g=2, c=256)
                    for c in range(2):
                        for gu in range(2):
                            b = base + gu * 2 + c
                            for kk in range(16):
                                MM(ps[b][:, :], v[:, kk, gu, c * 128:(c + 1) * 128], hnT[:, half * 16 + kk, :],
                                   half == 0 and kk == 0, half == 1 and kk == 15, [wsl_t[si], hnT_t], [ps_t[b]])
                for c in range(2):
                    k = state["ev"] % 2
                    state["ev"] += 1
                    ACT(sg[k][:, :], ps[base + c][:, :], AF.Silu, [ps_t[base + c]], [sg_t[k]])
                    TTo(hT[:, 2 * j + c, :], sg[k][:, :], ps[base + 2 + c][:, :], ALU.mult,
                        [sg_t[k], ps_t[base + 2 + c]], [hT_t[2 * j + c]])
            proj_tm(hT, lambda kc: hT_t[kc], 86, wd, f_dst, fname, t)

        def nstf():
            i = state["st"] % 6
            state["st"] += 1
            return i

        def nstb():
            i = state["sb"] % 4
            state["sb"] += 1
            return i

        def rope_store(bank, rbank, dst_scr, row0, t, dname):
            a, b1, b2 = nstf(), nstf(), nstf()
            ACT(stf[a], ps[bank][:, :], AF.Copy, [ps_t[bank]], [stf_t[a]])
            MM(ps[rbank][:, :], rotm[:, :], stf[a], True, True, [stf_t[a], const_t], [ps_t[rbank]])
            TTo(stf[b1], stf[a], cosb, ALU.mult, [stf_t[a], cs_t], [stf_t[b1]])
            TTo(stf[b2], ps[rbank][:, :], sinb, ALU.mult, [ps_t[rbank], cs_t], [stf_t[b2]])
            o = nstb()
            TTo(stb[o], stf[b1], stf[b2], ALU.add, [stf_t[b1], stf_t[b2]], [stb_t[o]])
            if isinstance(dst_scr, list):
                dst_scr, row0, dname = dst_scr[row0 // 1024], row0 % 1024, f"{dname}{row0 // 1024}"
            DMA(f"stb{o}", dst_scr[row0:row0 + 128, t * TT:(t + 1) * TT], stb[o], [stb_t[o]], [dt_(dname)])

        def proj_in(t):
            guard_big()
            DMA("cs", cosb, cos_d[:, t * TT:(t + 1) * TT], [], [cs_t])
            DMA("cs", sinb, sin_d[:, t * TT:(t + 1) * TT], [], [cs_t])
            for j in range(INC // 256):
                base = 4 * (j % 2)
                si = wslot()
                v = wsl[si][:, :].rearrange("p (k c) -> p k c", c=256)
                for half in range(2):
                    src = w_in[half * 2048:(half + 1) * 2048, j * 256:(j + 1) * 256].rearrange("(k p) c -> p k c", p=128)
                    DMA(f"w{si}", v[:, half * 16:(half + 1) * 16, :], src, [], [wsl_t[si]], queue="pool")
                col0 = j * 256
                if 4096 <= col0 < 6144:
                    for tg in range(4):
                        for kk in range(32):
                            MM(ps[base + tg][:, 0:256], hnT[:, kk, tg * 128:(tg + 1) * 128], v[:, kk, :],
                               kk == 0, kk == 31, [wsl_t[si], hnT_t], [ps_t[base + tg]])
                    for tg in range(4):
                        o = nstb()
                        r0 = t * TT + tg * 128
                        COPY("act" if tg % 2 == 0 else "dve", stb[o][:, 0:256], ps[base + tg][:, 0:256],
                             [ps_t[base + tg]], [stb_t[o]])
                        DMA(f"stb{o}", v_scr[t][tg * 128:(tg + 1) * 128, col0 - 4096:col0 - 4096 + 256], stb[o][:, 0:256],
                            [stb_t[o]], [dt_(f"v_scr{t}")])
                    continue
                for c in range(2):
                    for kk in range(32):
                        MM(ps[base + c][:, :], v[:, kk, c * 128:(c + 1) * 128], hnT[:, kk, :],
                           kk == 0, kk == 31, [wsl_t[si], hnT_t], [ps_t[base + c]])
                for c in range(2):
                    col = col0 + c * 128
                    bank = base + c
                    if col < 2048:
                        rope_store(bank, base + 2 + c, qT_scr, col, t, "qT_scr")
                    elif col < 4096:
                        rope_store(bank, base + 2 + c, kT_scr, col - 2048, t, "kT_scr")
                    elif col < 8192:
                        a = nstf()
                        row0 = col - 6144
                        COPY("act" if c == 0 else "dve", stf[a], ps[bank][:, :], [ps_t[bank]], [stf_t[a]])
                        DMA(f"stf{a}", xr_scr[row0 // 512][row0 % 512:row0 % 512 + 128, t * TT:(t + 1) * TT], stf[a],
                            [stf_t[a]], [dt_(f"xr_scr{row0 // 512}")])
                    elif col < 10240:
                        a, b = nstf(), nstf()
                        row0 = col - 8192
                        ACT(stf[a], ps[bank][:, :], AF.Copy, [ps_t[bank]], [stf_t[a]])
                        TTo(stf[b], stf[a], stf[a], ALU.mult, [stf_t[a]], [stf_t[b]])
                        TS(stf[b], stf[b], 0.044715, 1.0, ALU.mult, ALU.add, [stf_t[b]], [stf_t[b]])
                        TTo(stf[b], stf[b], stf[a], ALU.mult, [stf_t[a], stf_t[b]], [stf_t[b]])
                        ACT(stf[b], stf[b], AF.Sigmoid, [stf_t[b]], [stf_t[b]], scale=1.5957691216057308)
                        TTo(stf[b], stf[b], stf[a], ALU.mult, [stf_t[a], stf_t[b]], [stf_t[b]])
                        DMA(f"stf{b}", gg_scr[row0:row0 + 128, t * TT:(t + 1) * TT], stf[b], [stf_t[b]], [dt_("gg_scr")])
                    else:
                        a = nstf()
                        if col < 14336:
                            dst, row0, dn = sga_scr, col - 10240, "sga_scr"
                        else:
                            dst, row0, dn = sgb_scr, col - 14336, "sgb_scr"
                        ACT(stf[a], ps[bank][:, :], AF.Sigmoid, [ps_t[bank]], [stf_t[a]])
                        DMA(f"stf{a}", dst[row0:row0 + 128, t * TT:(t + 1) * TT], stf[a], [stf_t[a]], [dt_(dn)])

        def exchange():
            groups = [[2 * i, 2 * i + 1] for i in range(NCORES // 2)]
            for nm, srcs, dsts in (("kT", kT_scr, kT_all), ("v", v_scr, v_all), ("xr", xr_scr, xr_all)):
                for i, (src, dst) in enumerate(zip(srcs, dsts)):
                    P.dma("pool", f"cc_{nm}{i}",
                          lambda e, src=src, dst=dst: e.collective_compute("AllGather", ALU.bypass, replica_groups=groups,
                                                                           ins=[src.ap().opt()], outs=[dst.ap().opt()]),
                          reads=[dt_(f"{nm}_scr{i}")], writes=[dt_(f"{nm}_all{i}")], inc=1)

        def attention():
            guard_big()
            guard_wsl()
            units = [(h, qi) for h in range(NH) for qi in range(8)]

            def head_setup(h):
                s = h % 2
                kt, vv, qt = KT[s], VV[s], QT[s]
                hs = slice(h * 128, (h + 1) * 128)
                hi, hr = h // 8, (h % 8) * 128
                DMA(f"kvq{s}", kt[:, 0:1024], kT_all[hi][hr:hr + 128, :], [dt_(f"kT_all{hi}")], [kvq_t[s]])
                DMA(f"kvq{s}", kt[:, 1024:2048], kT_scr[hi][hr:hr + 128, :], [dt_(f"kT_scr{hi}")], [kvq_t[s]])
                for tc in range(2):
                    DMA(f"kvq{s}", vv[:, 4 * tc:4 * tc + 4, :], v_all[tc][0:512, hs].rearrange("(c p) d -> p c d", p=128),
                        [dt_(f"v_all{tc}")], [kvq_t[s]])
                    DMA(f"kvq{s}", vv[:, 8 + 4 * tc:12 + 4 * tc, :], v_scr[tc][0:512, hs].rearrange("(c p) d -> p c d", p=128),
                        [dt_(f"v_scr{tc}")], [kvq_t[s]])
                DMA(f"kvq{s}", qt, qT_scr[h * 128:(h + 1) * 128, :], [dt_("qT_scr")], [kvq_t[s]])
                G = ag[:, s, :]
                P.op("dve", lambda e: e.tensor_reduce(out=G[:, 128:136], in_=kt.rearrange("p (b k) -> p b k", k=256),
                                                      axis=AX.X, op=ALU.add), reads=[kvq_t[s]], writes=[ag_t[s]])
                TS(kmb[s], G[:, 128:136], 1.0 / 256.0, None, ALU.mult, None, [ag_t[s]], [kvq_t[s]])
                for qi in range(8):
                    MM(ps[7][:, qi * 8:(qi + 1) * 8], qt[:, qi * 128:(qi + 1) * 128], kmb[s], True, True, [kvq_t[s]], [ps_t[7]])
                TTo(G[:, 0:64], ps[7][:, 0:64], pastb[:, 0:64], ALU.add, [ps_t[7], const_t], [ag_t[s]])
                for qi in range(8):
                    c0 = 128 + 8 * (qi % 2)
                    P.op("dve", lambda e, c0=c0, qi=qi: e.max(out=G[:, c0:c0 + 8], in_=G[:, qi * 8:(qi + 1) * 8]),
                         reads=[ag_t[s]], writes=[ag_t[s]])
                    TS(G[:, 64 + qi * 8:72 + qi * 8], G[:, qi * 8:(qi + 1) * 8], G[:, c0 + 2:c0 + 3], None, ALU.is_ge, None,
                       [ag_t[s]], [ag_t[s]])
                TS(G[:, 64:128], G[:, 64:128], -NEG, NEG, ALU.mult, ALU.add, [ag_t[s]], [ag_t[s]])
                TTo(G[:, 64:128], G[:, 64:128], pastb[:, 0:64], ALU.add, [ag_t[s], const_t], [ag_t[s]])

            def stage_a2(idx):
                h, qi = units[idx]
                p = idx % 2
                nk = 1024 + (qi + 1) * 128
                mx = amx[:, p, :]
                ACT(PbL[p][:, 0:nk], SmL[p][:, 0:nk], AF.Exp, [Sm_t[p], amx_t[p]], [Pb_t[p]], bias=mx[:, 13:14], scale=SCALE)

            def stage_a1(idx):
                h, qi = units[idx]
                if qi == 0:
                    head_setup(h)
                s, p = h % 2, idx % 2
                kt, qt, G = KT[s], QT[s], ag[:, s, :]
                mx = amx[:, p, :]
                Sm_, Pb_ = SmL[p], PbL[p]
                nk = 1024 + (qi + 1) * 128
                npc = (nk + 511) // 512
                ob = 4 + qi // 2
                qs = qt[:, qi * 128:(qi + 1) * 128]
                for pc in range(npc):
                    w = min(512, nk - pc * 512)
                    MM(ps[4 * p + pc][:, 0:w], qs, kt[:, pc * 512:pc * 512 + w], True, True, [kvq_t[s]], [ps_t[4 * p + pc]])
                ncol = 0
                for blk in range(ob):
                    bank, off = 4 * p + blk // 2, (blk % 2) * 256
                    bcol = 64 + qi * 8 + blk
                    TS(Sm_[:, blk * 256:(blk + 1) * 256], ps[bank][:, off:off + 256], G[:, bcol:bcol + 1], None,
                       ALU.add, ALU.max, [ps_t[bank], ag_t[s]], [Sm_t[p], amx_t[p]], accum_out=mx[:, ncol:ncol + 1])
                    ncol += 1
                k0 = ob * 256
                if qi % 2 == 1:
                    bank, off = 4 * p + k0 // 512, k0 % 512
                    TS(Sm_[:, k0:k0 + 128], ps[bank][:, off:off + 128], 0.0, None, ALU.add, ALU.max,
                       [ps_t[bank]], [Sm_t[p], amx_t[p]], accum_out=mx[:, ncol:ncol + 1])
                    ncol += 1
                    k0 += 128
                bank, off = 4 * p + k0 // 512, k0 % 512
                TTo(Sm_[:, k0:k0 + 128], ps[bank][:, off:off + 128], tri[:, :], ALU.add, [ps_t[bank], const_t], [Sm_t[p]])
                P.op("dve", lambda e, ncol=ncol, k0=k0: e.tensor_reduce(out=mx[:, ncol:ncol + 1], in_=Sm_[:, k0:k0 + 128],
                                                                        axis=AX.X, op=ALU.max),
                     reads=[Sm_t[p]], writes=[amx_t[p]])
                ncol += 1
                P.op("dve", lambda e, ncol=ncol: e.tensor_reduce(out=mx[:, 12:13], in_=mx[:, 0:ncol], axis=AX.X, op=ALU.max),
                     reads=[amx_t[p]], writes=[amx_t[p]])
                TS(mx[:, 13:14], mx[:, 12:13], -SCALE, None, ALU.mult, None, [amx_t[p]], [amx_t[p]])

            def stage_b1(idx):
                h, qi = units[idx]
                p = idx % 2
                Pb_, PT_ = PbL[p], PTL[p]
                nk = 1024 + (qi + 1) * 128
                nch = nk // 128
                for c in range(nch):
                    bank = 4 * p + c // 8
                    pbv = ps[bank][:, :].bitcast(BF16)
                    TR(pbv[:, (c % 8) * 128:(c % 8 + 1) * 128], Pb_[:, c * 128:(c + 1) * 128], [Pb_t[p], const_t], [ps_t[bank]])
                COPY("act", PT_[:, 0:1024], ps[4 * p][:, :].bitcast(BF16), [ps_t[4 * p]], [PT_t[p]])
                n2 = (nch - 8) * 128
                COPY("act", PT_[:, 1024:1024 + n2], ps[4 * p + 1][:, :].bitcast(BF16)[:, 0:n2], [ps_t[4 * p + 1]], [PT_t[p]])

            def stage_b2(idx):
                h, qi = units[idx]
                s, p = h % 2, idx % 2
                vv = VV[s]
                PT_, rc_ = PTL[p], rcL[p]
                nch = (1024 + (qi + 1) * 128) // 128
                bo, br = 4 * p + 2, 4 * p + 3
                for c in range(nch):
                    MM(ps[bo][:, 0:128], vv[:, c, :], PT_[:, c * 128:(c + 1) * 128], c == 0, c == nch - 1,
                       [kvq_t[s], PT_t[p]], [ps_t[bo]])
                for c in range(nch):
                    MM(ps[br][:, 0:128], ones[:, :], PT_[:, c * 128:(c + 1) * 128], c == 0, c == nch - 1,
                       [const_t, PT_t[p]], [ps_t[br]])
                P.op("dve", lambda e: e.reciprocal(out=rc_, in_=ps[br][:, 0:128]), reads=[ps_t[br]], writes=[rc_t[p]])
                TTo(attnT[:, h, qi * 128:(qi + 1) * 128], ps[bo][:, 0:128], rc_, ALU.mult, [ps_t[bo], rc_t[p]], [attnT_t[h]])

            stage_a1(0)
            stage_a2(0)
            for i in range(len(units)):
                if i + 1 < len(units):
                    stage_a1(i + 1)
                stage_b1(i)
                if i + 1 < len(units):
                    stage_a2(i + 1)
                stage_b2(i)

        def lru_consts():
            L = lambda i: lsm[:, 16 * i:16 * (i + 1)]
            lam = lp[:, 7 * 16:8 * 16]
            ACT(L(0), lam, AF.Exp, [const_t], [lsm_t], scale=-1.0)
            TS(L(1), L(0), 2.0, None, ALU.add, None, [lsm_t], [lsm_t])
            P.op("dve", lambda e: e.reciprocal(out=L(1), in_=L(1)), reads=[lsm_t], writes=[lsm_t])
            TTo(L(1), L(1), L(0), ALU.mult, [lsm_t], [lsm_t])
            TTo(L(2), L(1), L(1), ALU.mult, [lsm_t], [lsm_t])
            TS(L(3), L(2), 1.0 / 9.0, 1.0 / 7.0, ALU.mult, ALU.add, [lsm_t], [lsm_t])
            TTo(L(3), L(3), L(2), ALU.mult, [lsm_t], [lsm_t])
            TS(L(3), L(3), 1.0 / 5.0, None, ALU.add, None, [lsm_t], [lsm_t])
            TTo(L(3), L(3), L(2), ALU.mult, [lsm_t], [lsm_t])
            TS(L(3), L(3), 1.0 / 3.0, None, ALU.add, None, [lsm_t], [lsm_t])
            TTo(L(3), L(3), L(2), ALU.mult, [lsm_t], [lsm_t])
            TS(L(3), L(3), 1.0, None, ALU.add, None, [lsm_t], [lsm_t])
            TTo(L(3), L(3), L(1), ALU.mult, [lsm_t], [lsm_t])
            TS(L(4), L(3), 16.0, None, ALU.mult, None, [lsm_t], [lsm_t])
            TS(L(5), L(3), -16.0, None, ALU.mult, None, [lsm_t], [lsm_t])
            TS(L(6), L(3), -32.0, None, ALU.mult, None, [lsm_t], [lsm_t])

        def lru():
            guard_wsl()
            L = lambda i, c: lsm[:, 16 * i + c:16 * i + c + 1]
            LPc = lambda i, c: lp[:, 16 * i + c:16 * i + c + 1]
            P.op("dve", lambda e: e.memset(xr[:, 0:3], 0.0), reads=[], writes=[xr_t])
            for c in range(16):
                rows = slice(c * 128, (c + 1) * 128)
                ci, cr = c // 4, (c % 4) * 128
                DMA("xr", xr[:, 3:1027], xr_all[ci][cr:cr + 128, :], [dt_(f"xr_all{ci}")], [xr_t])
                DMA("xr", xr[:, 1027:2051], xr_scr[ci][cr:cr + 128, :], [dt_(f"xr_scr{ci}")], [xr_t])
                TS(xr[:, 3:1027], xr[:, 3:1027], flag[:, 0:1], None, ALU.mult, None, [xr_t, const_t], [xr_t])
                for pc in range(4):
                    o0 = pc * 512
                    if pc >= 2:
                        DMA("ggb", ggb, gg_scr[c * 128:(c + 1) * 128, (pc - 2) * 512:(pc - 1) * 512], [dt_("gg_scr")], [ggb_t])
                    TS(ly, xr[:, o0:o0 + 512], LPc(0, c), LPc(4, c), ALU.mult, ALU.add, [xr_t, const_t], [ly_t])
                    for k in range(1, 4):
                        STT(ly, xr[:, o0 + k:o0 + k + 512], LPc(k, c), ly, ALU.mult, ALU.add, [xr_t, const_t, ly_t], [ly_t])
                    COPY("act", yb, ly, [ly_t], [yb_t])
                    MM(ps[6][:, :], wax[:, 0, c, :], yb, True, True, [yb_t, const_t], [ps_t[6]])
                    MM(ps[7][:, :], wax[:, 1, c, :], yb, True, True, [yb_t, const_t], [ps_t[7]])
                    ACT(lr, ps[6][:, :], AF.Sigmoid, [ps_t[6], const_t], [lr_t], bias=LPc(5, c))
                    ACT(li, ps[7][:, :], AF.Sigmoid, [ps_t[7], const_t], [li_t], bias=LPc(6, c))
                    ACT(lb1, lr, AF.Exp, [lr_t, lsm_t], [lb1_t], scale=L(6, c))
                    ACT(lb2, lr, AF.Tanh, [lr_t, lsm_t], [lb2_t], scale=L(4, c))
                    ACT(lr, lr, AF.Exp, [lr_t, lsm_t], [lr_t], scale=L(5, c))
                    STT(lb1, lb1, 1.0, lb2, ALU.add, ALU.mult, [lb1_t, lb2_t], [lb1_t])
                    ACT(lb1, lb1, AF.Sqrt, [lb1_t], [lb1_t])
                    TTo(ly, ly, li, ALU.mult, [ly_t, li_t], [ly_t])
                    TTo(ly, ly, lb1, ALU.mult, [ly_t, lb1_t], [ly_t])
                    if pc < 2:
                        TS(ly, ly, flag[:, 0:1], None, ALU.mult, None, [ly_t, const_t], [ly_t])
                    if pc == 0:
                        P.op("dve", lambda e: e.tensor_tensor_scan(out=lh, data0=lr, data1=ly, initial=0.0,
                                                                   op0=ALU.mult, op1=ALU.add),
                             reads=[lr_t, ly_t], writes=[lh_t])
                    else:
                        TS(lsm[:, 120:121], lh[:, 511:512], 1.0, None, ALU.mult, None, [lh_t], [lsc_t])
                        P.op("dve", lambda e: e.tensor_tensor_scan(out=lh, data0=lr, data1=ly, initial=lsm[:, 120:121],
                                                                   op0=ALU.mult, op1=ALU.add),
                             reads=[lr_t, ly_t, lsc_t], writes=[lh_t])
                    if pc >= 2:
                        TTo(yrecT[:, c, (pc - 2) * 512:(pc - 1) * 512], lh, ggb, ALU.mult, [lh_t, ggb_t], [yrecT_t[c]])

        def out_proj(t):
            guard_wsl()
            tok = slice(t * TT, (t + 1) * TT)
            nsg = 0
            for bi in range(D // 256):
                base = 4 * (bi % 2)
                si = wslot()
                v = wsl[si][:, :].rearrange("p (g k c) -> p g k c", g=2, c=256)
                for g, w in enumerate((w_ao, w_ro)):
                    DMA(f"w{si}", v[:, g, :, :], w[:, bi * 256:(bi + 1) * 256].rearrange("(k p) c -> p k c", p=128), [],
                        [wsl_t[si]], queue="pool")
                for g, (srcT, tl) in enumerate(((attnT, attnT_t), (yrecT, yrecT_t))):
                    for c in range(2):
                        for kk in range(16):
                            MM(ps[base + 2 * g + c][:, :], v[:, g, kk, c * 128:(c + 1) * 128], srcT[:, kk, tok],
                               kk == 0, kk == 15, [wsl_t[si], tl[kk]], [ps_t[base + 2 * g + c]])
                for c in range(2):
                    ch = 2 * bi + c
                    ia, ib = nsg % 4, (nsg + 1) % 4
                    nsg += 2
                    DMA(f"osg{ia}", osg[ia], sga_scr[ch * 128:(ch + 1) * 128, tok], [dt_("sga_scr")], [osg_t[ia]])
                    DMA(f"osg{ib}", osg[ib], sgb_scr[ch * 128:(ch + 1) * 128, tok], [dt_("sgb_scr")], [osg_t[ib]])
                    TTo(otA, ps[base + c][:, :], osg[ia], ALU.mult, [ps_t[base + c], osg_t[ia]], [otA_t])
                    TTo(otB, ps[base + 2 + c][:, :], osg[ib], ALU.mult, [ps_t[base + 2 + c], osg_t[ib]], [otB_t])
                    TTo(hnT[:, ch, :], otA, otB, ALU.add, [otA_t, otB_t], [hnT_t])
            proj_tm(hnT, lambda kc: hnT_t, 32, w_o, m_scr, "m", t)

        full = MODE == "full"
        for t in range(NTOK // TT):
            if full:
                norm_phase(t, x, None, None, 0.0, None, 0, "x", None, None)
                ffn(t, w1g, w1u, w1d, f1_scr, "f1")
                norm_phase(t, x, f1_scr, gs["ffn1_post_g"], 0.5, x1_scr, 2, "x", "f1", "x1")
            else:
                norm_phase(t, x, None, None, 0.0, None, 2, "x", None, None)
            proj_in(t)
        exchange()
        lru_consts()
        attention()
        lru()
        for t in range(NTOK // TT):
            out_proj(t)
        for t in range(NTOK // TT):
            if full:
                norm_phase(t, x1_scr, m_scr, gs["mix_post_g"], 1.0, x2_scr, 4, "x1", "m", "x2")
                ffn(t, w2g, w2u, w2d, f2_scr, "f2")
                norm_phase(t, x2_scr, f2_scr, gs["ffn2_post_g"], 0.5, out, None, "x2", "f2", "out", final=True)
            else:
                norm_phase(t, x, m_scr, gs["mix_post_g"], 1.0, out, None, "x", "m", "out", final=True)

        P.emit()
        self.nc_in_names = list(self.in_names)
        return nc


_CACHE = {}


def _get_nc():
    key = ("nc", MODE, NCORES, DEBUG)
    if key not in _CACHE:
        b = Builder()
        _CACHE[key] = (b.build(), b.in_names)
    return _CACHE[key]


def _host_consts(s):
    import ml_dtypes
    bf = ml_dtypes.bfloat16
    c = {}
    c["ident"] = np.eye(128, dtype=np.float32).astype(bf)
    c["ones"] = np.ones((128, 128), dtype=np.float32).astype(bf)
    rot = np.zeros((128, 128), dtype=np.float32)
    for d in range(64):
        rot[d + 64, d] = -1.0
        rot[d, d + 64] = 1.0
    c["rotm"] = rot
    p = np.arange(128)
    c["tri"] = np.where(p[None, :] <= p[:, None], 0.0, NEG).astype(np.float32)
    inv_freq = (np.float32(10000.0) ** (-np.arange(0, HD, 2, dtype=np.float32) / np.float32(HD))).astype(np.float32)
    pos = (np.arange(NTOK, dtype=np.float32) + np.float32(s * NTOK)).astype(np.float32)
    ang = (pos[:, None] * inv_freq[None, :]).astype(np.float32)
    cosT = np.cos(ang).astype(np.float32).T
    sinT = np.sin(ang).astype(np.float32).T
    c["cosT"] = np.ascontiguousarray(np.concatenate([cosT, cosT], axis=0))
    c["sinT"] = np.ascontiguousarray(np.concatenate([sinT, sinT], axis=0))
    pb = np.full((8, 8), NEGBIG, dtype=np.float32)
    for qi in range(8):
        ob = 4 + qi // 2
        lo = 0 if s == 1 else 4
        pb[qi, lo:ob] = 0.0
    c["pastb"] = np.ascontiguousarray(np.broadcast_to(pb.reshape(1, 64), (128, 64))).astype(np.float32)
    c["flag"] = np.full((128, 1), float(s), dtype=np.float32)
    return c


def kernel(**inputs):
    nc, in_names = _get_nc()
    f32 = lambda a: np.asarray(a, dtype=np.float32)
    x = np.ascontiguousarray(f32(inputs["x"]))
    B, S, _ = x.shape
    g_names = ["ffn1_pre_g", "ffn1_post_g", "mix_pre_g", "mix_post_g", "ffn2_pre_g", "ffn2_post_g"]
    shared = {}
    shared["gT"] = np.ascontiguousarray(np.concatenate([f32(inputs[n]).reshape(32, 128).T for n in g_names], axis=1))
    for n in g_names:
        shared[n] = np.ascontiguousarray(f32(inputs[n]).reshape(1, D))
    for n in ["ffn1_w_gate", "ffn1_w_up", "ffn1_w_down", "ffn2_w_gate", "ffn2_w_up", "ffn2_w_down", "w_in",
              "w_attn_out", "w_rec_out", "w_o", "rg_w_a", "rg_w_x"]:
        if n not in in_names:
            continue
        a = f32(inputs[n])
        shared[n] = np.ascontiguousarray(a.reshape(a.shape[1:]))
    cw = f32(inputs["conv_w"]).reshape(4, LW)
    cols = [cw[k] for k in range(4)] + [f32(inputs[n]).reshape(LW) for n in ("conv_b", "rg_b_a", "rg_b_x", "lru_lambda")]
    shared["lruP"] = np.ascontiguousarray(np.concatenate([v.reshape(16, 128).T for v in cols], axis=1))
    consts = [_host_consts(0), _host_consts(1)]
    in_maps = []
    for c in range(NCORES):
        b, s = c // 2, c % 2
        m = dict(shared)
        m.update(consts[s])
        m["x"] = np.ascontiguousarray(x[b, s * NTOK:(s + 1) * NTOK, :])
        in_maps.append({k: v for k, v in m.items() if k in in_names})
    res = run_bass_kernel_spmd(nc, in_maps, core_ids=list(range(NCORES)))
    outp = np.zeros((B, S, D), dtype=np.float32)
    for c in range(NCORES):
        b, s = c // 2, c % 2
        outp[b, s * NTOK:(s + 1) * NTOK, :] = res.results[c]["out"]
    if DEBUG:
        _CACHE["dbg"] = res.results
    return outp
```
